# Optimizing a Trainium2 kernel written in Bass

```python
import jax, jax.numpy as jnp
from jax import lax
import numpy as np

D_MODEL = 1024
BATCH = 1
SEQ = 16384
DEPTH = 2

D_MIX = D_MODEL
W_A = D_MIX // 4
W_B = D_MIX // 4
W_C = D_MIX // 4
W_D = D_MIX // 4
SHORT_CONV = 3
POOL_WINDOWS = (2, 4, 8, 16)
N_POOL_GROUPS = len(POOL_WINDOWS)
POOL_GROUP_DIM = W_B // N_POOL_GROUPS
CONF_WIDTH = 31
SGU_CHUNK = 128
N_HEADS_D = 4
SGU_HEAD_DIM = W_D // N_HEADS_D
IN_SIZES = (W_A, W_A, W_A, W_A,
            W_B, W_B,
            W_C, W_C, W_C,
            W_D, W_D, W_D)
D_IN = sum(IN_SIZES)
EPS = 1e-6

kernel_name = "hybrid_parallel_conv_pool_conformer_sgu"


def rms_norm(x, g):
    xf = x.astype(jnp.float32)
    y = xf * lax.rsqrt(jnp.mean(xf * xf, axis=-1, keepdims=True) + EPS)
    return (y * g.astype(jnp.float32)).astype(x.dtype)


def layer_norm(x, g, b):
    xf = x.astype(jnp.float32)
    mu = jnp.mean(xf, axis=-1, keepdims=True)
    var = jnp.mean(jnp.square(xf - mu), axis=-1, keepdims=True)
    y = (xf - mu) * lax.rsqrt(var + EPS)
    return (y * g.astype(jnp.float32) + b.astype(jnp.float32)).astype(x.dtype)


def causal_dwconv(x, w):
    k = w.shape[0]
    xp = jnp.pad(x, ((0, 0), (k - 1, 0), (0, 0)))
    return lax.conv_general_dilated(
        xp, w[:, None, :].astype(x.dtype), window_strides=(1,), padding='VALID',
        dimension_numbers=('NWC', 'WIO', 'NWC'), feature_group_count=x.shape[-1])


def short_conv_mixer(b_gate, c_gate, xt, w_conv):
    return b_gate * causal_dwconv(c_gate * xt, w_conv)


def multiscale_pool_mixer(p, w_pool, pool_scale):
    bsz, t_len, _ = p.shape
    pf = p.astype(jnp.float32)
    t_idx = jnp.arange(t_len)
    outs = []
    for grp, w in zip(jnp.split(pf, N_POOL_GROUPS, axis=-1), POOL_WINDOWS):
        cs = jnp.pad(jnp.cumsum(grp, axis=1), ((0, 0), (w, 0), (0, 0)))
        wsum = cs[:, w:] - cs[:, :t_len]
        count = jnp.minimum(t_idx + 1, w).astype(jnp.float32)[None, :, None]
        outs.append(wsum / count - grp)
    pooled = jnp.stack(outs, axis=2).astype(p.dtype)
    y = jnp.einsum('btgc,gcd->btgd', pooled, w_pool).reshape(bsz, t_len, W_B)
    return y * pool_scale


def conformer_conv_mixer(a, gl, w_dw, b_dw, ln_g, ln_b, w_pw2, b_pw2):
    h = a * jax.nn.sigmoid(gl)
    h = causal_dwconv(h, w_dw) + b_dw
    h = jax.nn.silu(layer_norm(h, ln_g, ln_b))
    return h @ w_pw2 + b_pw2


def chunked_sgu_mixer(u, v, ln_g, ln_b, w_s, b_s):
    u = jax.nn.gelu(u)
    v = layer_norm(jax.nn.gelu(v), ln_g, ln_b)
    bsz, t_len, _ = v.shape
    n_chunks = t_len // SGU_CHUNK
    vc = v.reshape(bsz, n_chunks, SGU_CHUNK, N_HEADS_D, SGU_HEAD_DIM)
    mask = jnp.tril(jnp.ones((SGU_CHUNK, SGU_CHUNK), dtype=bool))
    ws = jnp.where(mask[None], w_s, 0)
    mixed = jnp.einsum('hst,bnthc->bnshc', ws, vc) + b_s.T[None, None, :, :, None]
    return u * mixed.reshape(bsz, t_len, W_D)


def setup_inputs(seed: int = 0) -> dict:
    key = jax.random.key(seed)
    ks = jax.random.split(key, 24)
    f32 = jnp.float32
    nrm = lambda k, shape, s: jax.random.normal(k, shape, f32) * s
    return {
        "x": nrm(ks[0], (BATCH, SEQ, D_MODEL), 1.0),
        "c": nrm(ks[1], (BATCH, D_MODEL), 1.0),
        "norm_g": 1.0 + nrm(ks[2], (DEPTH, D_MODEL), 0.02),
        "w_ada": nrm(ks[3], (DEPTH, D_MODEL, 3 * D_MODEL), 0.5 * D_MODEL ** -0.5),
        "b_ada": nrm(ks[4], (DEPTH, 3 * D_MODEL), 0.01),
        "w_in": nrm(ks[5], (DEPTH, D_MODEL, D_IN), D_MODEL ** -0.5),
        "w_conv_a": nrm(ks[6], (DEPTH, SHORT_CONV, W_A), SHORT_CONV ** -0.5),
        "w_pool": nrm(ks[7], (DEPTH, N_POOL_GROUPS, POOL_GROUP_DIM, POOL_GROUP_DIM), POOL_GROUP_DIM ** -0.5),
        "pool_scale": 1.0 + nrm(ks[8], (DEPTH, W_B), 0.02),
        "w_dw_c": nrm(ks[9], (DEPTH, CONF_WIDTH, W_C), CONF_WIDTH ** -0.5),
        "b_dw_c": nrm(ks[10], (DEPTH, W_C), 0.01),
        "ln_g_c": 1.0 + nrm(ks[11], (DEPTH, W_C), 0.02),
        "ln_b_c": nrm(ks[12], (DEPTH, W_C), 0.01),
        "w_pw2_c": nrm(ks[13], (DEPTH, W_C, W_C), W_C ** -0.5),
        "b_pw2_c": nrm(ks[14], (DEPTH, W_C), 0.01),
        "ln_g_d": 1.0 + nrm(ks[15], (DEPTH, W_D), 0.02),
        "ln_b_d": nrm(ks[16], (DEPTH, W_D), 0.01),
        "w_s_d": nrm(ks[17], (DEPTH, N_HEADS_D, SGU_CHUNK, SGU_CHUNK), 0.5 * SGU_CHUNK ** -0.5),
        "b_s_d": 1.0 + nrm(ks[18], (DEPTH, N_HEADS_D, SGU_CHUNK), 0.02),
        "w_out": nrm(ks[19], (DEPTH, D_MIX, D_MODEL), D_MIX ** -0.5),
        "final_g": 1.0 + nrm(ks[20], (D_MODEL,), 0.02),
    }


def reference(x, c, norm_g, w_ada, b_ada, w_in, w_conv_a, w_pool, pool_scale,
              w_dw_c, b_dw_c, ln_g_c, ln_b_c, w_pw2_c, b_pw2_c,
              ln_g_d, ln_b_d, w_s_d, b_s_d, w_out, final_g):
    split_points = list(np.cumsum(IN_SIZES)[:-1])
    c_act = jax.nn.silu(c)
    for l in range(DEPTH):
        mod = c_act @ w_ada[l] + b_ada[l]
        shift, scale, gate = jnp.split(mod, 3, axis=-1)
        h = rms_norm(x, norm_g[l]) * (1.0 + scale[:, None, :]) + shift[:, None, :]
        z = h @ w_in[l]
        (a_b, a_c, a_x, a_g, b_p, b_g, c_a, c_gl, c_g,
         d_u, d_v, d_g) = jnp.split(z, split_points, axis=-1)
        y_a = short_conv_mixer(a_b, a_c, a_x, w_conv_a[l]) * jax.nn.silu(a_g)
        y_b = multiscale_pool_mixer(b_p, w_pool[l], pool_scale[l]) * jax.nn.silu(b_g)
        y_c = conformer_conv_mixer(c_a, c_gl, w_dw_c[l], b_dw_c[l], ln_g_c[l], ln_b_c[l],
                                   w_pw2_c[l], b_pw2_c[l]) * jax.nn.silu(c_g)
        y_d = chunked_sgu_mixer(d_u, d_v, ln_g_d[l], ln_b_d[l], w_s_d[l], b_s_d[l]) * jax.nn.silu(d_g)
        y = jnp.concatenate([y_a, y_b, y_c, y_d], axis=-1) @ w_out[l]
        x = x + gate[:, None, :] * y
    return rms_norm(x, final_g)
```

```python
import numpy as np
from contextlib import ExitStack
import concourse.bass as bass
import concourse.mybir as mybir
from concourse.bass_utils import run_bass_kernel_spmd

F32 = mybir.dt.float32
BF16 = mybir.dt.bfloat16
AF = mybir.ActivationFunctionType
ALU = mybir.AluOpType

NCORE = 8
DM = 1024
SEQ = 16384
TC = SEQ // NCORE
HALO = 128
NT = TC + HALO
KC = 8
DIN = 3072
DEPTH = 2
EPS = 1e-6
NB = 256
BLOCKS = [(0, 128)] + [(128 + NB * i, NB) for i in range(TC // NB)]
G_AB, G_AC, G_AX, G_AG, G_BP, G_BG, G_CA, G_CGL, G_CG, G_DU, G_DV, G_DG = range(12)
PV_NG, PV_BADA, PV_WCA, PV_INVW, PV_PSC, PV_WDW, PV_BDW, PV_LGC, PV_LBC, PV_BPW, PV_FG, PV_C = \
    0, 8, 32, 38, 40, 42, 104, 106, 108, 110, 112, 120
PV_LGD, PV_LBD = 128, 130
NPV = 136


class TT:
    __slots__ = ("name", "w", "r")

    def __init__(self, name):
        self.name = name
        self.w = None
        self.r = []


class Op:
    __slots__ = ("eng", "fn", "dma", "key", "deps", "needs_inc", "val", "sem")


class Sched:
    ENGS = ("pe", "act", "dve", "pool", "sp")

    def __init__(self):
        self.ops = []

    def add(self, eng, fn, reads=(), writes=(), dma=False, key=None):
        op = Op()
        op.eng, op.fn, op.dma, op.key = eng, fn, dma, key
        op.needs_inc = False
        op.val = 0
        op.sem = None
        deps = {}
        for t in reads:
            if t.w is not None:
                deps.setdefault(id(t.w), [t.w, False])[1] = True
        for t in writes:
            if t.w is not None:
                deps.setdefault(id(t.w), [t.w, False])
            for r in t.r:
                deps.setdefault(id(r), [r, False])
        keep = []
        for d, raw in deps.values():
            if d is op:
                continue
            if d.dma or dma or d.eng != eng:
                keep.append(d)
            elif eng != "pe":
                keep.append(d)
        op.deps = keep
        for d in keep:
            if not d.dma:
                d.needs_inc = True
        for t in writes:
            t.w = op
            t.r = []
        for t in reads:
            if t.w is not op:
                t.r.append(op)
        self.ops.append(op)
        return op

    def emit(self, nc, es):
        esem = {e: es.enter_context(nc.semaphore("sem_" + e)) for e in self.ENGS}
        cnt = {e: 0 for e in self.ENGS}
        ksem = {}
        kcnt = {}
        for op in self.ops:
            if op.dma:
                k = id(op.key)
                if k not in ksem:
                    ksem[k] = es.enter_context(nc.semaphore("dsem%d" % len(ksem)))
                    kcnt[k] = 0
                kcnt[k] += 16
                op.sem = ksem[k]
                op.val = kcnt[k]
            else:
                if op.needs_inc:
                    cnt[op.eng] += 1
                op.sem = esem[op.eng]
                op.val = cnt[op.eng]
        ops = self.ops

        def body(e):
            def f(eng):
                waited = {}
                for op in ops:
                    if op.eng != e:
                        continue
                    for d in op.deps:
                        k = id(d.sem)
                        if waited.get(k, 0) < d.val:
                            eng.wait_ge(d.sem, d.val)
                            waited[k] = d.val
                    if op.fn is None:
                        continue
                    ins = op.fn(eng)
                    if op.dma:
                        ins.then_inc(op.sem, 16)
                    elif op.needs_inc:
                        ins.then_inc(op.sem, 1)
            return f

        with nc.Block() as block:
            block.tensor(body("pe"))
            block.scalar(body("act"))
            block.vector(body("dve"))
            block.gpsimd(body("pool"))
            block.sync(body("sp"))


def build_program(depth=DEPTH, dbg=False):
    nc = bass.Bass("TRN2", target_bir_lowering=False)
    es = ExitStack()
    S = Sched()

    def dram(name, shape, kind="ExternalInput"):
        return nc.dram_tensor(name, list(shape), F32, kind=kind).ap()

    xT_d = dram("xT", [DM, NT])
    pv_d = dram("pv", [DEPTH, 128, NPV])
    bc_d = dram("bc", [DEPTH, 128, 256])
    mats_d = dram("mats", [DEPTH, 128, 1408])
    win_d = dram("w_in", [DEPTH, DM, DIN])
    wout_d = dram("w_out", [DEPTH, DM, DM])
    wada_d = dram("w_ada", [DEPTH, DM, DIN])
    cst_d = dram("cst", [128, 128 + 32 + 1])
    out_d = dram("outT", [DM, TC], kind="ExternalOutput")

    def sb(name, shape, dt=F32):
        return es.enter_context(nc.sbuf_tensor(name, list(shape), dt))

    def ps(name, shape):
        return es.enter_context(nc.psum_tensor(name, list(shape), F32))

    xres = sb("xres", [128, KC, NT])
    win = sb("win", [128, KC, DIN], BF16)
    wout = sb("wout", [128, KC, DM], BF16)
    ht = [sb("ht%d" % i, [128, KC, NB], BF16) for i in range(2)]
    yt = sb("yt", [128, KC, NB], BF16)
    stg = sb("stg", [128, KC, NB], BF16)
    rstd = sb("rstd", [128, NB])
    rstdf = sb("rstdf", [128, NB])
    th = sb("th", [128, 2, NB], BF16)
    hc = sb("hc", [128, 2, 32 + NB], BF16)
    sgc = sb("sgc", [128, 2, NB], BF16)
    cv = sb("cv", [128, 2, NB])
    cvb = sb("cvb", [128, 2, NB], BF16)
    cvsq = sb("cvsq", [128, 2, NB], BF16)
    msq = sb("msq", [128, NB])
    rstdc = sb("rstdc", [128, NB])
    hs = cvsq
    gv = sb("gv", [128, 2, 256])
    mv = sb("mv", [128, 2, 8])
    mv2 = sb("mv2", [128, 2, 2])
    rsd = sb("rsd", [128, 2])
    vt = sb("vt", [128, 2, 256], BF16)
    sa1 = sb("sa1", [128, 2, NB], BF16)
    sa2 = sb("sa2", [128, 2, NB], BF16)
    sa3 = sb("sa3", [128, 2, NB], BF16)
    sa4 = sb("sa4", [128, 2, NB], BF16)
    ua = sb("ua", [128, 2, 2 + NB + 6], BF16)
    pb = sb("pb", [128, 2, 16 + NB])
    sgb = sb("sgb", [128, 2, NB], BF16)
    sA = sb("sA", [128, 2, 16 + NB])
    sB = sb("sB", [128, 2, 16 + NB])
    pld = sb("pld", [128, 2, NB], BF16)
    pv = sb("pvs", [128, DEPTH, NPV])
    bcs2 = [sb("bcs%d" % i, [128, 256]) for i in range(2)]
    bst2 = sb("bst2", [128, 2, 128])
    bzr_hi = sb("bzr_hi", [128, 256], BF16)
    bzr_lo = sb("bzr_lo", [128, 256], BF16)
    e0 = sb("e0", [128, 128], BF16)
    matsb2 = [sb("matsb%d" % i, [128, 1408], BF16) for i in range(2)]
    wst = sb("wst", [128, 4, 128], BF16)
    dgc = sb("dgc", [128, 62, 128], BF16)
    dga = sb("dga", [128, 6, 128], BF16)
    wdwb = sb("wdwb", [128, 68], BF16)
    cst = sb("csts", [128, 161])
    identb = sb("identb", [128, 128], BF16)
    onesb = sb("onesb", [128, 128], BF16)
    ones256 = sb("ones256", [128, 128], BF16)
    junk = sb("junk", [128, 128])
    cact = sb("cact", [128, KC], BF16)
    mod = sb("mod", [128, DEPTH, 24])
    gs = sb("gs", [128, DEPTH, KC])
    shb = sb("shb", [128, DEPTH, KC], BF16)
    bz = sb("bz", [128, 24])
    hbz = sb("hbz", [128, 2])
    hbzca = sb("hbzca", [128, 2])
    epsc = sb("epsc", [128, 1])
    zp = [ps("zp%d" % i, [128, 2, NB]) for i in range(3)]
    pst = ps("pst", [128, 2, NB])
    misc = [ps("misc%d" % i, [128, 2, NB]) for i in range(2)]
    dvp = ps("dvp", [128, 2, 256])
    mixp = ps("mixp", [128, 2, NB])

    T = {}

    def tt(name):
        if name not in T:
            T[name] = TT(name)
        return T[name]

    XT = [tt("xt%d" % b) for b in range(len(BLOCKS))]
    WIN = [tt("win%d" % g) for g in range(12)]
    WOUT = [tt("wout%d" % g) for g in range(4)]
    HT = [[tt("ht%d_%d" % (p, k)) for k in range(KC)] for p in range(2)]
    ZP = [tt("zp%d" % i) for i in range(3)]
    MISC = [tt("misc%d" % i) for i in range(2)]
    state = {"zp": 0, "misc": 0}

    def next_zp():
        i = state["zp"]
        state["zp"] = (i + 1) % 3
        return zp[i], ZP[i]

    def next_misc():
        i = state["misc"]
        state["misc"] = (i + 1) % 2
        return misc[i], MISC[i]

    add = S.add
    mult, addop, sub, powop = ALU.mult, ALU.add, ALU.subtract, ALU.pow

    add("sp", lambda e: e.dma_start(out=pv[:], in_=pv_d.rearrange("l p c -> p l c")),
        writes=[tt("pv")], dma=True, key=tt("pv"))
    add("sp", lambda e: e.dma_start(out=cst[:], in_=cst_d), writes=[tt("cst")], dma=True, key=tt("cst"))
    xT_v = xT_d.rearrange("(k p) t -> p k t", p=128)
    def load_x(b, after=()):
        t0, n = BLOCKS[b]
        add("sp", lambda e: e.dma_start(out=xres[:, :, t0:t0 + n], in_=xT_v[:, :, t0:t0 + n]),
            reads=list(after), writes=[XT[b]], dma=True, key=XT[b])
    load_x(0)
    add("dve", lambda e: e.memset(onesb[:], 1.0), writes=[tt("onesb")])
    add("dve", lambda e: e.memset(ones256[:], 1.0 / 256.0), writes=[tt("ones256")])
    add("dve", lambda e: e.memset(e0[:], 0.0), writes=[tt("e0")])
    add("dve", lambda e: e.memset(e0[0:1, :], 1.0), writes=[tt("e0")])
    add("dve", lambda e: e.memset(bzr_hi[:], 0.0), writes=[tt("bzr_hi")])
    add("dve", lambda e: e.memset(bzr_lo[:], 0.0), writes=[tt("bzr_lo")])
    add("dve", lambda e: e.tensor_copy(identb[:], cst[:, 0:128]), reads=[tt("cst")], writes=[tt("identb")])
    add("dve", lambda e: e.memset(hc[:], 0.0), writes=[tt("hc")])
    add("dve", lambda e: e.memset(ua[:], 0.0), writes=[tt("ua")])
    add("dve", lambda e: e.memset(pb[:], 0.0), writes=[tt("pb")])
    add("dve", lambda e: e.memset(sA[:], 0.0), writes=[tt("sA")])
    add("dve", lambda e: e.memset(sB[:], 0.0), writes=[tt("sB")])
    add("act", lambda e: e.activation(cact[:], pv[:, 0, PV_C:PV_C + 8], AF.Silu),
        reads=[tt("pv")], writes=[tt("cact")])
    add("dve", lambda e: e.memset(epsc[:], float(EPS)), writes=[tt("epsc")])

    def load_w(dsrc, l, g):
        src = dsrc[l].rearrange("(k p) c -> p k c", p=128)[:, :, g * 256:(g + 1) * 256]
        add("pool", lambda e: e.dma_start(out=win[:, :, g * 256:(g + 1) * 256], in_=src),
            writes=[WIN[g]], dma=True, key=WIN[g])

    def load_wout(l, g):
        src = wout_d[l].rearrange("(k p) c -> p k c", p=128)[:, :, g * 256:(g + 1) * 256]
        add("pool", lambda e: e.dma_start(out=wout[:, :, g * 256:(g + 1) * 256], in_=src),
            writes=[WOUT[g]], dma=True, key=WOUT[g])

    YT_ALL = [tt("yt0"), tt("yt2"), tt("yt4"), tt("yt6")]
    loaded_w = set()
    loaded_wout = set()

    def ada_piece(l, q, use_yt=False, wslot=None, part="both"):
        if wslot is not None:
            buf, BT = wout[:, :, wslot * 256:(wslot + 1) * 256], [WOUT[wslot]]
        elif use_yt:
            buf, BT = yt[:], YT_ALL
        else:
            buf, BT = stg[:], [tt("stg")]
        src = wada_d[l].rearrange("(k p) c -> p k c", p=128)[:, :, q * 256:(q + 1) * 256]
        if part in ("both", "issue"):
            add("pool", lambda e: e.dma_start(out=buf, in_=src), writes=BT, dma=True, key=BT[0])
        if part == "issue":
            return
        zt, ZT = next_zp()
        zrow = zt.rearrange("p a b -> p (a b)")

        def fn(e):
            ins = None
            for kc in range(KC):
                ins = e.matmul(zrow[:, 0:256], cact[:, kc:kc + 1].broadcast_to([128, 128]), buf[:, kc, :],
                               start=(kc == 0), stop=(kc == KC - 1))
            return ins
        add("pe", fn, reads=[tt("cact")] + BT, writes=[ZT])
        for m in range(2):
            add("dve", (lambda m=m: lambda e: e.scalar_tensor_tensor(
                junk[:, 0:128], zrow[:, m * 128:(m + 1) * 128], pv[:, l, PV_BADA + 2 * q + m:PV_BADA + 2 * q + m + 1],
                cst[:, 0:128], addop, mult, accum_out=mod[:, l, 2 * q + m:2 * q + m + 1]))(),
                reads=[ZT, tt("pv"), tt("cst")], writes=[tt("junk"), tt("mod%d_%d" % (l, q // 4))])

    def ada_finish_scale(l):
        add("dve", lambda e: e.scalar_tensor_tensor(gs[:, l, :], mod[:, l, 8:16], 1.0,
                                                    pv[:, l, PV_NG:PV_NG + 8], addop, mult),
            reads=[tt("mod%d_1" % l), tt("pv")], writes=[tt("gs%d" % l)])

    def ada_finish_shift(l):
        add("dve", lambda e: e.tensor_copy(shb[:, l, :], mod[:, l, 0:8]),
            reads=[tt("mod%d_0" % l)], writes=[tt("shb%d" % l)])

    def early_consts():
        l = 0
        bcs, matsb = bcs2[0], matsb2[0]
        add("sp", lambda e: e.dma_start(out=bcs[:], in_=bc_d[l]), writes=[tt("bcs0")], dma=True, key=tt("bcs0"))
        add("pool", lambda e: e.dma_start(out=matsb[:], in_=mats_d[l]), writes=[tt("matsb0")], dma=True, key=tt("matsb0"))

    def preload_w(g):
        load_w(win_d, 0, g)
        loaded_w.add(g)

    FIRST_USE = [G_CGL, G_CA, G_CG, G_BP, G_BG, G_DV, G_DU, G_DG, G_AC, G_AX, G_AG, G_AB]

    def startup_seq():
        early_consts()
        load_layer_consts(0)
        ada_piece(0, 4, use_yt=False)
        ada_piece(0, 5, use_yt=True)
        preload_w(FIRST_USE[0])
        ada_piece(0, 6, use_yt=False)
        preload_w(FIRST_USE[1])
        ada_piece(0, 7, use_yt=True)
        ada_finish_scale(0)
        pt0, SP0, _ = rms_sq(0, 0)
        rms_sqrt(pt0, SP0, rstd, "rstd", BLOCKS[0][1])
        rms_apply(0, 0, 0)
        load_x(1, after=[tt("gs0")])
        preload_w(FIRST_USE[2])
        ada_piece(0, 0, use_yt=False)
        preload_w(FIRST_USE[3])
        ada_piece(0, 1, use_yt=True)
        preload_w(FIRST_USE[4])
        ada_piece(0, 2, use_yt=False)
        preload_w(FIRST_USE[5])
        ada_piece(0, 3, use_yt=True)
        ada_finish_shift(0)
        load_x(2, after=[tt("shb0")])
        preload_w(FIRST_USE[6])
        for g_ in FIRST_USE[7:]:
            preload_w(g_)
        for q_ in (8, 9, 10, 11):
            ada_piece(0, q_, wslot=q_ - 8, part="issue")
    ada_todo = [(1, q) for q in (4, 5, 6, 7, 0, 1, 2, 3, 8, 9, 10, 11)]

    def ada_next(cnt=1):
        for _ in range(cnt):
            if not ada_todo:
                return
            l_, q_ = ada_todo.pop(0)
            ada_piece(l_, q_)
            if l_ == 1 and q_ == 7:
                ada_finish_scale(1)
            if l_ == 1 and q_ == 3:
                ada_finish_shift(1)

    def load_layer_consts_dma(l):
        bcs, matsb = bcs2[l % 2], matsb2[l % 2]
        add("sp", lambda e: e.dma_start(out=bcs[:], in_=bc_d[l]), writes=[tt("bcs%d" % (l % 2))], dma=True, key=tt("bcs%d" % (l % 2)))
        add("pool", lambda e: e.dma_start(out=matsb[:], in_=mats_d[l]), writes=[tt("matsb%d" % (l % 2))], dma=True,
            key=tt("matsb%d" % (l % 2)))

    def build_dgc(l):
        add("dve", lambda e: e.tensor_copy(wdwb[:, 0:62], pv[:, l, PV_WDW:PV_WDW + 62]),
            reads=[tt("pv")], writes=[tt("wdwb")])
        add("dve", lambda e: e.tensor_tensor(dgc[:], identb[:].unsqueeze(1).broadcast_to([128, 62, 128]),
                                              wdwb[:, 0:62].unsqueeze(2).broadcast_to([128, 62, 128]), mult),
            reads=[tt("identb"), tt("wdwb")], writes=[tt("dgc")])

    def load_layer_consts(l):
        bcs, matsb = bcs2[l % 2], matsb2[l % 2]
        bst_v = bcs[:, 0:256].rearrange("p (j s) -> p j s", j=2)
        add("dve", lambda e: e.tensor_tensor(wst[:], matsb[:, 0:512].rearrange("p (h s) -> p h s", h=4),
                                             matsb[:, 512:640].unsqueeze(1).broadcast_to([128, 4, 128]), mult),
            reads=[tt("matsb%d" % (l % 2))], writes=[tt("wst")])
        rsp, RSP = next_zp()
        rsp2 = rsp.rearrange("p a b -> p (a b)")
        add("pe", lambda e: e.matmul(rsp2[:, 0:512], onesb[:], wst[:].rearrange("p h s -> p (h s)"), start=True, stop=True),
            reads=[tt("wst"), tt("onesb")], writes=[RSP])
        for j in range(2):
            for hl in range(2):
                h = 2 * j + hl
                add("dve", (lambda j=j, hl=hl, h=h: lambda e: e.scalar_tensor_tensor(
                    bst2[hl * 64:(hl + 1) * 64, j, :], rsp2[hl * 64:(hl + 1) * 64, h * 128:(h + 1) * 128],
                    pv[hl * 64:(hl + 1) * 64, l, PV_LBD + j:PV_LBD + j + 1], bst_v[hl * 64:(hl + 1) * 64, j, :],
                    mult, addop))(),
                    reads=[RSP, tt("pv"), tt("bcs%d" % (l % 2))], writes=[tt("bst2")])
        if l == 0:
            build_dgc(l)
        add("dve", lambda e: e.tensor_copy(wdwb[:, 62:68], pv[:, l, PV_WCA:PV_WCA + 6]),
            reads=[tt("pv")], writes=[tt("wdwa")])
        add("dve", lambda e: e.tensor_tensor(dga[:], identb[:].unsqueeze(1).broadcast_to([128, 6, 128]),
                                              wdwb[:, 62:68].unsqueeze(2).broadcast_to([128, 6, 128]), mult),
            reads=[tt("identb"), tt("wdwa")], writes=[tt("dga")])

    wpool_vs = [m_[:, 640:896].rearrange("p (j m) -> p j m", j=2) for m_ in matsb2]
    wpw2_vs = [m_[:, 896:1408].rearrange("p (k m) -> p k m", k=2) for m_ in matsb2]

    def bias_group(l, g):
        zt, ZT = next_zp()
        zrow = zt.rearrange("p a b -> p (a b)")

        def fn(e):
            ins = None
            for kc in range(KC):
                ins = e.matmul(zrow[:, 0:256], shb[:, l, kc:kc + 1].broadcast_to([128, 128]),
                               win[:, kc, g * 256:(g + 1) * 256], start=(kc == 0), stop=(kc == KC - 1))
            return ins
        add("pe", fn, reads=[tt("shb%d" % l), WIN[g]], writes=[ZT])
        if g == G_DV:
            add("dve", lambda e: e.tensor_copy(bzr_hi[0:1, :], zrow[0:1, 0:256]), reads=[ZT], writes=[tt("bzr_hi")])
            add("dve", lambda e: e.tensor_tensor(rstd[0:1, 0:256], zrow[0:1, 0:256], bzr_hi[0:1, :], sub),
                reads=[ZT, tt("bzr_hi")], writes=[tt("rstd")])
            add("dve", lambda e: e.tensor_copy(bzr_lo[0:1, :], rstd[0:1, 0:256]), reads=[tt("rstd")], writes=[tt("bzr_lo")])
        else:
            for m in range(2):
                add("dve", (lambda m=m: lambda e: e.scalar_tensor_tensor(
                    junk[:, 0:128], zrow[:, m * 128:(m + 1) * 128], 1.0, cst[:, 0:128], mult, mult,
                    accum_out=bz[:, 2 * g + m:2 * g + m + 1]))(),
                    reads=[ZT, tt("cst")], writes=[tt("junk"), tt("bz%d" % g)])
            if g == G_CGL:
                add("dve", lambda e: e.tensor_scalar(hbz[:], bz[:, 2 * g:2 * g + 2], 0.5, None, mult),
                    reads=[tt("bz%d" % g)], writes=[tt("hbz")])
            if g == G_CA:
                add("dve", lambda e: e.tensor_scalar(hbzca[:], bz[:, 2 * g:2 * g + 2], 0.5, None, mult),
                    reads=[tt("bz%d" % g)], writes=[tt("hbzca")])

    stores = []
    NBLK = len(BLOCKS)
    seq = [(l, b) for l in range(depth) for b in range(NBLK)]

    def rms_sq(b, par, bank=None, half=0, act_only=False, skip_act=False):
        t0, n = BLOCKS[b]
        htp = ht[par]
        if not skip_act:
            add("act", lambda e: e.activation(htp[:, :, 0:n], xres[:, :, t0:t0 + n], AF.Square),
                reads=[XT[b]], writes=HT[par])
        if act_only:
            return None
        if bank is None:
            sp_, SP_ = next_misc()
        else:
            sp_, SP_ = bank
        pt = sp_[:, half, :]

        def fn(e):
            ins = None
            for kc in range(KC):
                ins = e.matmul(pt[:, 0:n], onesb[:], htp[:, kc, 0:n], start=(kc == 0), stop=(kc == KC - 1))
            return ins
        add("pe", fn, reads=HT[par] + [tt("onesb")], writes=[SP_])
        return pt, SP_, sp_

    def rms_sqrt(pt, SP_, dst, dname, n):
        add("act", lambda e: e.activation(dst[:, 0:n], pt[:, 0:n], AF.Ln, bias=epsc[:, 0:1], scale=1.0 / DM),
            reads=[SP_, tt("epsc")], writes=[tt(dname)])
        add("act", lambda e: e.activation(dst[:, 0:n], dst[:, 0:n], AF.Exp, scale=-0.5),
            reads=[tt(dname)], writes=[tt(dname)])

    def rms_apply(l, b, par, kcs=range(KC)):
        t0, n = BLOCKS[b]
        htp = ht[par]
        for kc in kcs:
            add("dve", (lambda kc=kc: lambda e: e.scalar_tensor_tensor(
                htp[:, kc, 0:n], xres[:, kc, t0:t0 + n], gs[:, l, kc:kc + 1], rstd[:, 0:n], mult, mult))(),
                reads=[XT[b], tt("rstd"), tt("gs%d" % l)], writes=[HT[par][kc]])

    def fin_apply_store(b):
        t0, n = BLOCKS[b]
        for kc in range(KC):
            add("dve", (lambda kc=kc: lambda e: e.scalar_tensor_tensor(
                xres[:, kc, t0:t0 + n], xres[:, kc, t0:t0 + n], pv[:, 0, PV_FG + kc:PV_FG + kc + 1], rstdf[:, 0:n],
                mult, mult))(),
                reads=[XT[b], tt("rstdf"), tt("pv")], writes=[XT[b]])
        ov = out_d.rearrange("(k p) t -> p k t", p=128)[:, :, t0 - HALO:t0 - HALO + n]
        st = tt("store%d" % b)
        add("sp", lambda e: e.dma_start(out=ov, in_=xres[:, :, t0:t0 + n]),
            reads=[XT[b]], writes=[st], dma=True, key=st)
        stores.append(st)

    MIXB = (mixp, tt("mixp"))

    def pipe_next(i):
        if i + 1 >= len(seq):
            return False
        l, b = seq[i]
        if seq[i + 1][0] != l:
            return False
        if l == depth - 1 and b == 0:
            return False
        return True

    def block(i, mode="full"):
        l, b = seq[i]
        t0, n = BLOCKS[b]
        par = i % 2
        htp = ht[par]
        ns = n // 128
        first = (b == 0)
        last_layer = (l == depth - 1)
        last_blk = (b == NBLK - 1)
        nxt = seq[i + 1] if i + 1 < len(seq) else None
        reload_w = last_blk and (l + 1 < depth)
        do_fin = last_layer and b >= 2
        hreads = list(HT[par])
        W = 16 + n
        wpool_v, wpw2_v = wpool_vs[l % 2], wpw2_vs[l % 2]
        st8 = {}

        def ada_pt(k):
            if l != 0 or last_blk:
                return
            if first:
                if k == 0:
                    for q_ in (8, 9, 10, 11):
                        ada_piece(0, q_, wslot=q_ - 8, part="process")
                    for g_ in range(4):
                        load_wout(0, g_)
                        loaded_wout.add(g_)
            elif (k == 0) or (k == 1 and b <= 5):
                ada_next(1)

        def inproj(g):
            if first:
                if l == 0 and g not in loaded_w:
                    load_w(win_d, 0, g)
                bias_group(l, g)
            zt, ZT = next_zp()

            def fn(e):
                ins = None
                for j in range(2):
                    for kc in range(KC):
                        ins = e.matmul(zt[:, j, 0:n], win[:, kc, (2 * g + j) * 128:(2 * g + j + 1) * 128],
                                       htp[:, kc, 0:n], start=(kc == 0), stop=(kc == KC - 1))
                return ins
            add("pe", fn, reads=hreads + [WIN[g]], writes=[ZT])
            if reload_w:
                load_w(win_d, l + 1, g)
            return zt, ZT

        def prefetch_table(func, slot):
            dst = (mv[:, 0, 6:7], mv[:, 0, 7:8], mv[:, 1, 6:7])[slot]
            add("act", lambda e: e.activation(dst, epsc[:, 0:1], func), reads=[tt("epsc")], writes=[tt("dum%d" % slot)])

        def act2(zt, ZT, dst, dname, func, g, col=0, scale=1.0, biast=None, extra_reads=()):
            for j in range(2):
                bias_ap = biast[:, j:j + 1] if biast is not None else bz[:, 2 * g + j:2 * g + j + 1]
                add("act", (lambda j=j, bias_ap=bias_ap: lambda e: e.activation(
                    dst[:, j, col:col + n], zt[:, j, 0:n], func, bias=bias_ap, scale=scale))(),
                    reads=[ZT, tt("bz%d" % g)] + list(extra_reads), writes=[tt(dname)])

        def S_glu():
            zt, ZT = inproj(G_CGL)
            act2(zt, ZT, th, "th", AF.Tanh, G_CGL, scale=0.5, biast=hbz, extra_reads=[tt("hbz")])
            zt, ZT = inproj(G_CA)
            act2(zt, ZT, sa3, "sa3", AF.Identity, G_CA, scale=0.5, biast=hbzca, extra_reads=[tt("hbzca")])
            add("dve", lambda e: e.scalar_tensor_tensor(hc[:, :, 32:32 + n], th[:, :, 0:n], 1.0, sa3[:, :, 0:n], addop, mult),
                reads=[tt("th"), tt("sa3")], writes=[tt("hc")])

        def S_ssnxt():
            if nxt is not None:
                st8["nx"] = rms_sq(nxt[1], (i + 1) % 2, bank=MIXB, half=0, skip_act=True)
            if do_fin:
                rms_sq(b - 1, (i - 1) % 2, act_only=True)

        def S_cg():
            zt, ZT = inproj(G_CG)
            act2(zt, ZT, sgc, "sgc", AF.Silu, G_CG)

        def S_c31():
            cvp, CVP = next_misc()

            def fn_c31(e):
                ins = None
                for j in range(2):
                    for k in range(31):
                        ins = e.matmul(cvp[:, j, 0:n], dgc[:, j * 31 + k, :], hc[:, j, 2 + k:2 + k + n],
                                       start=(k == 0), stop=(k == 30))
                return ins
            add("pe", fn_c31, reads=[tt("hc"), tt("dgc")], writes=[CVP])
            for j in range(2):
                add("act", (lambda j=j: lambda e: e.activation(cvb[:, j, 0:n], cvp[:, j, 0:n], AF.Identity,
                                                               bias=pv[:, l, PV_BDW + j:PV_BDW + j + 1]))(),
                    reads=[CVP, tt("pv")], writes=[tt("cvb")])
                add("act", (lambda j=j: lambda e: e.activation(cvsq[:, j, 0:n], cvp[:, j, 0:n], AF.Square,
                                                               bias=pv[:, l, PV_BDW + j:PV_BDW + j + 1]))(),
                    reads=[CVP, tt("pv")], writes=[tt("cvsq")])

        def S_dv():
            if first:
                if l == 0 and G_DV not in loaded_w:
                    load_w(win_d, 0, G_DV)
                bias_group(l, G_DV)

            def fn_dv(e):
                ins = None
                for s in range(ns):
                    for kc in range(KC):
                        ins = e.matmul(dvp[:, s, :], htp[:, kc, s * 128:(s + 1) * 128],
                                       win[:, kc, G_DV * 256:(G_DV + 1) * 256], start=(kc == 0), stop=False)
                    e.matmul(dvp[:, s, :], e0[:], bzr_hi[:], start=False, stop=False)
                    ins = e.matmul(dvp[:, s, :], e0[:], bzr_lo[:], start=False, stop=True)
                return ins
            add("pe", fn_dv, reads=hreads + [WIN[G_DV], tt("bzr_hi"), tt("bzr_lo"), tt("e0")], writes=[tt("dvp")])
            if reload_w:
                load_w(win_d, l + 1, G_DV)
            add("act", lambda e: e.activation(gv[:, 0:ns, :], dvp[:, 0:ns, :], AF.Gelu_apprx_tanh),
                reads=[tt("dvp")], writes=[tt("gv")])
            for s in range(ns):
                add("dve", (lambda s=s: lambda e: e.bn_stats(mv[:, s, 0:6], gv[:, s, :]))(),
                    reads=[tt("gv")], writes=[tt("mv")])
            for s in range(ns):
                add("dve", (lambda s=s: lambda e: e.bn_aggr(mv2[:, s, :], mv[:, s, 0:6]))(),
                    reads=[tt("mv")], writes=[tt("mv2")])

        def S_du():
            zt, ZT = inproj(G_DU)
            act2(zt, ZT, sa1, "sa1", AF.Gelu_apprx_tanh, G_DU)
            prefetch_table(AF.Ln, 1)

        def S_stats():
            def fn_st(e):
                ins = None
                for j in range(2):
                    ins = e.matmul(pst[:, 0, 0:n], ones256[:], cvb[:, j, 0:n], start=(j == 0), stop=(j == 1))
                for j in range(2):
                    ins = e.matmul(pst[:, 1, 0:n], ones256[:], cvsq[:, j, 0:n], start=(j == 0), stop=(j == 1))
                return ins
            add("pe", fn_st, reads=[tt("cvb"), tt("cvsq"), tt("ones256")], writes=[tt("pst")])
            add("act", lambda e: e.activation(msq[:, 0:n], pst[:, 0, 0:n], AF.Square), reads=[tt("pst")], writes=[tt("msq")])
            add("dve", lambda e: e.scalar_tensor_tensor(rstdc[:, 0:n], pst[:, 1, 0:n], float(EPS), msq[:, 0:n], addop, sub),
                reads=[tt("pst"), tt("msq")], writes=[tt("rstdc")])

        def S_b():
            zt, ZT = inproj(G_BP)
            act2(zt, ZT, pb, "pb", AF.Identity, G_BP, col=16)
            zt, ZT = inproj(G_BG)
            act2(zt, ZT, sgb, "sgb", AF.Silu, G_BG)
            prefetch_table(AF.Gelu_apprx_tanh, 0)
            add("dve", lambda e: e.tensor_tensor(sA[:, :, 2:W], pb[:, :, 2:W], pb[:, :, 1:W - 1], addop),
                reads=[tt("pb")], writes=[tt("sA")])
            add("dve", lambda e: e.tensor_tensor(sB[:, :, 4:W], sA[:, :, 4:W], sA[:, :, 2:W - 2], addop),
                reads=[tt("sA")], writes=[tt("sB")])
            add("dve", lambda e: e.tensor_tensor(sA[:, 1, 8:W], sB[:, 1, 8:W], sB[:, 1, 4:W - 4], addop),
                reads=[tt("sB"), tt("sA")], writes=[tt("sA")])
            add("dve", lambda e: e.tensor_tensor(sB[64:128, 1, 16:W], sA[64:128, 1, 16:W], sA[64:128, 1, 8:W - 8], addop),
                reads=[tt("sA"), tt("sB")], writes=[tt("sB")])
            if b == 1:
                cf = cst[:, 128:160].rearrange("p (j t) -> p j t", j=2)
                add("dve", lambda e: e.tensor_tensor(sA[:, :, 16:32], sA[:, :, 16:32], cf, mult),
                    reads=[tt("sA"), tt("cst")], writes=[tt("sA")])
                add("dve", lambda e: e.tensor_tensor(sB[:, :, 16:32], sB[:, :, 16:32], cf, mult),
                    reads=[tt("sB"), tt("cst")], writes=[tt("sB")])
            for j in range(2):
                for hf, src, srct in ((0, sA, "sA"), (1, sB, "sB")):
                    p0, p1 = hf * 64, hf * 64 + 64
                    add("dve", (lambda j=j, p0=p0, p1=p1, src=src: lambda e: e.scalar_tensor_tensor(
                        pld[p0:p1, j, 0:n], src[p0:p1, j, 16:W], pv[p0:p1, l, PV_INVW + j:PV_INVW + j + 1],
                        pb[p0:p1, j, 16:W], mult, sub))(),
                        reads=[tt(srct), tt("pb"), tt("pv")], writes=[tt("pld")])

        def S_finss():
            if do_fin:
                st8["fin"] = rms_sq(b - 1, (i - 1) % 2, bank=MIXB, half=1, skip_act=True)

        def S_burst():
            add("act", lambda e: e.activation(rstdc[:, 0:n], rstdc[:, 0:n], AF.Ln),
                reads=[tt("rstdc")], writes=[tt("rstdc")])
            add("act", lambda e: e.activation(rsd[:, 0:ns], mv2[:, 0:ns, 1], AF.Ln, bias=epsc[:, 0:1]),
                reads=[tt("mv2"), tt("epsc")], writes=[tt("rsd")])
            add("act", lambda e: e.activation(rstdc[:, 0:n], rstdc[:, 0:n], AF.Exp, scale=-0.5),
                reads=[tt("rstdc")], writes=[tt("rstdc")])
            add("act", lambda e: e.activation(rsd[:, 0:ns], rsd[:, 0:ns], AF.Exp, scale=-0.5),
                reads=[tt("rsd")], writes=[tt("rsd")])
            if "nx" in st8:
                rms_sqrt(st8["nx"][0], st8["nx"][1], rstd, "rstd", BLOCKS[nxt[1]][1])
            if "fin" in st8:
                rms_sqrt(st8["fin"][0], st8["fin"][1], rstdf, "rstdf", BLOCKS[b - 1][1])
            prefetch_table(AF.Silu, 2)
            add("dve", lambda e: e.tensor_tensor(cv[:, :, 0:n], cvb[:, :, 0:n],
                                                 pst[:, 0:1, 0:n].broadcast_to([128, 2, n]), sub),
                reads=[tt("cvb"), tt("pst")], writes=[tt("cv")])
            add("dve", lambda e: e.tensor_tensor(cv[:, :, 0:n], cv[:, :, 0:n],
                                                 rstdc[:, 0:n].unsqueeze(1).broadcast_to([128, 2, n]), mult),
                reads=[tt("cv"), tt("rstdc")], writes=[tt("cv")])
            for s in range(ns):
                add("dve", (lambda s=s: lambda e: e.tensor_scalar(vt[:, s, :], gv[:, s, :], mv2[:, s, 0:1],
                                                                  rsd[:, s:s + 1], sub, mult))(),
                    reads=[tt("gv"), tt("mv2"), tt("rsd")], writes=[tt("vt")])
            for j in range(2):
                add("act", (lambda j=j: lambda e: e.activation(hs[:, j, 0:n], cv[:, j, 0:n], AF.Silu,
                                                               bias=pv[:, l, PV_LBC + j:PV_LBC + j + 1],
                                                               scale=pv[:, l, PV_LGC + j:PV_LGC + j + 1]))(),
                    reads=[tt("cv"), tt("pv")], writes=[tt("cvsq")])

        def S_dg():
            zt, ZT = inproj(G_DG)
            act2(zt, ZT, sa2, "sa2", AF.Silu, G_DG)
            add("dve", lambda e: e.tensor_tensor(sa1[:, :, 0:n], sa1[:, :, 0:n], sa2[:, :, 0:n], mult),
                reads=[tt("sa1"), tt("sa2")], writes=[tt("sa1")])

        def S_pool():
            plp, PLP = next_misc()

            def fn_pl(e):
                ins = None
                for j in range(2):
                    ins = e.matmul(plp[:, j, 0:n], wpool_v[:, j, :], pld[:, j, 0:n], start=True, stop=True)
                return ins
            add("pe", fn_pl, reads=[tt("pld"), tt("matsb%d" % (l % 2))], writes=[PLP])
            for j in range(2):
                add("dve", (lambda j=j: lambda e: e.scalar_tensor_tensor(
                    yt[:, 2 + j, 0:n], plp[:, j, 0:n], pv[:, l, PV_PSC + j:PV_PSC + j + 1], sgb[:, j, 0:n], mult, mult))(),
                    reads=[PLP, tt("pv"), tt("sgb")], writes=[tt("yt2")])

        def S_ac():
            zt, ZT = inproj(G_AC)
            act2(zt, ZT, sa3, "sa3", AF.Identity, G_AC)

        def S_sgu():
            def fn_sgu(e):
                ins = None
                for s in range(ns):
                    for h in range(4):
                        j, hl = h // 2, h % 2
                        ins = e.matmul(mixp[hl * 64:(hl + 1) * 64, j, s * 128:(s + 1) * 128],
                                       vt[:, s, h * 64:(h + 1) * 64], wst[:, h, :], start=True, stop=True)
                return ins
            add("pe", fn_sgu, reads=[tt("vt"), tt("wst")], writes=[tt("mixp")])
            for j in range(2):
                add("dve", (lambda j=j: lambda e: e.scalar_tensor_tensor(
                    gv[:, j, 0:n].rearrange("p (s q) -> p s q", q=128), mixp[:, j, 0:n].rearrange("p (s q) -> p s q", q=128),
                    pv[:, l, PV_LGD + j:PV_LGD + j + 1], bst2[:, j, :].unsqueeze(1).broadcast_to([128, ns, 128]),
                    mult, addop))(),
                    reads=[tt("mixp"), tt("bst2"), tt("pv")], writes=[tt("gv")])
            add("dve", lambda e: e.tensor_tensor(yt[:, 6:8, 0:n], gv[:, :, 0:n], sa1[:, :, 0:n], mult),
                reads=[tt("sa1"), tt("gv")], writes=[tt("yt6")])

        def S_ax():
            zt, ZT = inproj(G_AX)
            for j in range(2):
                add("dve", (lambda j=j, zt=zt: lambda e: e.scalar_tensor_tensor(
                    ua[:, j, 2:2 + n], zt[:, j, 0:n], bz[:, 2 * G_AX + j:2 * G_AX + j + 1], sa3[:, j, 0:n], addop, mult))(),
                    reads=[ZT, tt("bz%d" % G_AX), tt("sa3")], writes=[tt("ua")])

        def S_pw():
            pwp, PWP = next_misc()

            def fn_pw(e):
                ins = None
                for jo in range(2):
                    for k2 in range(2):
                        ins = e.matmul(pwp[:, jo, 0:n], wpw2_v[:, k2, jo * 128:(jo + 1) * 128], hs[:, k2, 0:n],
                                       start=(k2 == 0), stop=(k2 == 1))
                return ins
            add("pe", fn_pw, reads=[tt("cvsq"), tt("matsb%d" % (l % 2))], writes=[PWP])
            for j in range(2):
                add("dve", (lambda j=j: lambda e: e.scalar_tensor_tensor(
                    yt[:, 4 + j, 0:n], pwp[:, j, 0:n], pv[:, l, PV_BPW + j:PV_BPW + j + 1], sgc[:, j, 0:n], addop, mult))(),
                    reads=[PWP, tt("pv"), tt("sgc")], writes=[tt("yt4")])

        def S_ag_ab():
            zt, ZT = inproj(G_AG)
            act2(zt, ZT, sa4, "sa4", AF.Silu, G_AG)
            zt, ZT = inproj(G_AB)
            for j in range(2):
                add("dve", (lambda j=j, zt=zt: lambda e: e.scalar_tensor_tensor(
                    sa4[:, j, 0:n], zt[:, j, 0:n], bz[:, 2 * G_AB + j:2 * G_AB + j + 1], sa4[:, j, 0:n], addop, mult))(),
                    reads=[ZT, tt("bz%d" % G_AB), tt("sa4")], writes=[tt("sa4")])

        def S_c3():
            c3p, C3P = next_misc()

            def fn_c3(e):
                ins = None
                for j in range(2):
                    for k in range(3):
                        ins = e.matmul(c3p[:, j, 0:n], dga[:, j * 3 + k, :], ua[:, j, k:k + n],
                                       start=(k == 0), stop=(k == 2))
                return ins
            add("pe", fn_c3, reads=[tt("ua"), tt("dga")], writes=[C3P])
            add("dve", lambda e: e.tensor_tensor(yt[:, 0:2, 0:n], c3p[:, :, 0:n], sa4[:, :, 0:n], mult),
                reads=[C3P, tt("sa4")], writes=[tt("yt0")])

        def S_tail():
            if not last_blk:
                for buf, name, hw in ((ua, "ua", 2), (pb, "pb", 16), (hc, "hc", 32)):
                    if first:
                        add("pool", (lambda buf=buf, hw=hw: lambda e: e.tensor_scalar(
                            buf[:, :, 0:hw], buf[:, :, n:n + hw], cst[:, 160:161], None, mult))(),
                            reads=[tt(name), tt("cst")], writes=[tt(name)])
                    else:
                        add("pool", (lambda buf=buf, hw=hw: lambda e: e.tensor_copy(buf[:, :, 0:hw], buf[:, :, n:n + hw]))(),
                            reads=[tt(name)], writes=[tt(name)])
            if l == 0 and b + 3 < NBLK:
                load_x(b + 3, after=[tt("yt0")])

        def S_out(fo2s=range(4)):
            yreads = [tt("yt0"), tt("yt2"), tt("yt4"), tt("yt6")]
            for fo2 in fo2s:
                if first and l == 0 and fo2 not in loaded_wout:
                    load_wout(0, fo2)
                zt, ZT = next_zp()

                def fn_o(e, fo2=fo2, zt=zt):
                    ins = None
                    for j in range(2):
                        fo = 2 * fo2 + j
                        for kc in range(KC):
                            ins = e.matmul(zt[:, j, 0:n], wout[:, kc, fo * 128:(fo + 1) * 128], yt[:, kc, 0:n],
                                           start=(kc == 0), stop=(kc == KC - 1))
                    return ins
                add("pe", fn_o, reads=yreads + [WOUT[fo2]], writes=[ZT])
                if reload_w:
                    load_wout(l + 1, fo2)
                for j in range(2):
                    fo = 2 * fo2 + j
                    add("dve", (lambda j=j, fo=fo, zt=zt: lambda e: e.scalar_tensor_tensor(
                        xres[:, fo, t0:t0 + n], zt[:, j, 0:n], mod[:, l, 16 + fo:17 + fo], xres[:, fo, t0:t0 + n],
                        mult, addop))(),
                        reads=[ZT, tt("mod%d_2" % l), XT[b]], writes=[XT[b]])
                if nxt is not None and not pipe_next(i):
                    rms_apply(nxt[0], nxt[1], (i + 1) % 2, kcs=(2 * fo2, 2 * fo2 + 1))

        if mode == "head_a":
            S_glu()
            return
        if mode == "head_b":
            S_c31()
            return
        skip_head = (i > 0 and pipe_next(i - 1))
        pipe = pipe_next(i)
        if last_layer and first:
            rms_sq(nxt[1], (i + 1) % 2, act_only=True)
            S_glu()
            S_ssnxt()
            zt, ZT = inproj(G_BP)
            act2(zt, ZT, pb, "pb", AF.Identity, G_BP, col=16)
            rms_sqrt(st8["nx"][0], st8["nx"][1], rstd, "rstd", BLOCKS[nxt[1]][1])
            S_ac()
            S_ax()
            S_tail()
            rms_apply(nxt[0], nxt[1], (i + 1) % 2)
            for g_ in (G_CG, G_BG, G_DV, G_DU, G_DG, G_AG, G_AB):
                bias_group(l, g_)
            return
        if nxt is not None:
            rms_sq(nxt[1], (i + 1) % 2, act_only=True)
        if not skip_head:
            S_glu()
        S_ssnxt()
        S_cg()
        if not skip_head:
            S_c31()
        S_b()
        S_dv()
        ada_pt(0)
        S_du()
        S_stats()
        S_finss()
        S_burst()
        if last_blk and l + 1 < depth:
            assert skip_head
            build_dgc(l + 1)
        if pipe:
            rms_apply(nxt[0], nxt[1], (i + 1) % 2)
        S_dg()
        S_pool()
        S_ac()
        S_sgu()
        S_ax()
        S_pw()
        S_ag_ab()
        S_c3()
        ada_pt(1)
        S_tail()
        if pipe:
            block(i + 1, mode="head_a")
            S_out((0, 1))
            block(i + 1, mode="head_b")
            S_out((2, 3))
        else:
            S_out()
        if do_fin:
            fin_apply_store(b - 1)

    startup_seq()
    for i in range(len(seq)):
        if seq[i][1] == 0 and seq[i][0] > 0:
            load_layer_consts(seq[i][0])
        if seq[i] == (0, 2) and depth > 1:
            load_layer_consts_dma(1)
        block(i)
    add("act", lambda e: e.activation(mv[:, 1, 7:8], epsc[:, 0:1], AF.Ln), reads=[tt("epsc")], writes=[tt("dum3")])
    ptf, SPF, _ = rms_sq(NBLK - 1, (len(seq) - 1) % 2)
    rms_sqrt(ptf, SPF, rstdf, "rstdf", BLOCKS[NBLK - 1][1])
    fin_apply_store(NBLK - 1)
    add("sp", None, reads=stores)
    S.emit(nc, es)
    es.close()
    return nc


def _pack_inputs(inp):
    f = lambda a: np.ascontiguousarray(np.asarray(a, dtype=np.float32))
    x = f(inp["x"])[0]
    c = f(inp["c"])[0]
    col = lambda v: v.reshape(-1, 128).T
    pvs, bcs, matss = [], [], []
    tril_t = (np.arange(128)[:, None] <= np.arange(128)[None, :]).astype(np.float32)
    for l in range(DEPTH):
        pv = np.zeros((128, NPV), np.float32)
        pv[:, PV_NG:PV_NG + 8] = col(f(inp["norm_g"])[l])
        pv[:, PV_BADA:PV_BADA + 24] = col(f(inp["b_ada"])[l])
        wca = f(inp["w_conv_a"])[l]
        wdw = f(inp["w_dw_c"])[l]
        for j in range(2):
            pv[:, PV_WCA + 3 * j:PV_WCA + 3 * j + 3] = wca[:, j * 128:(j + 1) * 128].T
            pv[:, PV_WDW + 31 * j:PV_WDW + 31 * j + 31] = wdw[:, j * 128:(j + 1) * 128].T
        pv[0:64, PV_INVW] = 1.0 / 2
        pv[64:128, PV_INVW] = 1.0 / 4
        pv[0:64, PV_INVW + 1] = 1.0 / 8
        pv[64:128, PV_INVW + 1] = 1.0 / 16
        pv[:, PV_PSC:PV_PSC + 2] = col(f(inp["pool_scale"])[l])
        pv[:, PV_BDW:PV_BDW + 2] = col(f(inp["b_dw_c"])[l])
        pv[:, PV_LGC:PV_LGC + 2] = col(f(inp["ln_g_c"])[l])
        pv[:, PV_LBC:PV_LBC + 2] = col(f(inp["ln_b_c"])[l])
        pv[:, PV_BPW:PV_BPW + 2] = col(f(inp["b_pw2_c"])[l])
        pv[:, PV_FG:PV_FG + 8] = col(f(inp["final_g"]))
        pv[:, PV_C:PV_C + 8] = col(c)
        pv[:, PV_LGD:PV_LGD + 2] = col(f(inp["ln_g_d"])[l])
        pv[:, PV_LBD:PV_LBD + 2] = col(f(inp["ln_b_d"])[l])
        pvs.append(pv)
        bc = np.zeros((128, 256), np.float32)
        bs = f(inp["b_s_d"])[l]
        for j in range(2):
            for hl in range(2):
                bc[hl * 64:(hl + 1) * 64, j * 128:(j + 1) * 128] = bs[2 * j + hl][None, :]
        bcs.append(bc)
        m = np.zeros((128, 1408), np.float32)
        ws = f(inp["w_s_d"])[l]
        for h in range(4):
            m[:, h * 128:(h + 1) * 128] = ws[h].T
        m[:, 512:640] = tril_t
        wp = f(inp["w_pool"])[l]
        for j in range(2):
            for gl in range(2):
                m[gl * 64:(gl + 1) * 64, 640 + j * 128 + gl * 64:640 + j * 128 + (gl + 1) * 64] = wp[2 * j + gl]
        w2 = f(inp["w_pw2_c"])[l]
        for k in range(2):
            m[:, 896 + k * 256:896 + (k + 1) * 256] = w2[k * 128:(k + 1) * 128, :]
        matss.append(m)
    pv_a, bc_a, mats_a = np.stack(pvs), np.stack(bcs), np.stack(matss)
    w_in, w_out, w_ada = f(inp["w_in"]), f(inp["w_out"]), f(inp["w_ada"])
    in_maps = []
    winv = np.array([2, 4, 8, 16], np.float32)
    for core in range(NCORE):
        xt = np.zeros((DM, NT), np.float32)
        lo = core * TC - HALO
        if lo < 0:
            xt[:, HALO:] = x[0:TC].T
        else:
            xt[:, :] = x[lo:lo + NT].T
        cst = np.zeros((128, 161), np.float32)
        cst[:, 0:128] = np.eye(128, dtype=np.float32)
        cf = np.ones((128, 2, 16), np.float32)
        if core == 0:
            for j in range(2):
                for hf in range(2):
                    w = winv[2 * j + hf]
                    tpos = np.arange(16, dtype=np.float32)
                    cf[hf * 64:(hf + 1) * 64, j, :] = w / np.minimum(tpos + 1, w)
        cst[:, 128:160] = cf.reshape(128, 32)
        cst[:, 160] = 0.0 if core == 0 else 1.0
        in_maps.append({"xT": np.ascontiguousarray(xt), "pv": pv_a, "bc": bc_a, "mats": mats_a,
                        "w_in": w_in, "w_out": w_out, "w_ada": w_ada, "cst": cst})
    return in_maps


_NC_CACHE = {}


def kernel(**inputs):
    in_maps = _pack_inputs(inputs)
    if "nc" not in _NC_CACHE:
        _NC_CACHE["nc"] = build_program()
    nc = _NC_CACHE["nc"]
    res = run_bass_kernel_spmd(nc, in_maps, core_ids=list(range(NCORE)))
    out = np.empty((1, SEQ, DM), np.float32)
    for core in range(NCORE):
        out[0, core * TC:(core + 1) * TC, :] = res.results[core]["outT"].T
    return out
```

```python
import numpy as np
from contextlib import ExitStack
import concourse.bass as bass
import concourse.mybir as mybir
from concourse.bass_utils import run_bass_kernel_spmd

F32 = mybir.dt.float32
BF16 = mybir.dt.bfloat16
AF = mybir.ActivationFunctionType
ALU = mybir.AluOpType

NCORE = 8
DM = 1024
SEQ = 16384
TC = SEQ // NCORE
HALO = 128
NT = TC + HALO
KC = 8
DIN = 3072
DEPTH = 2
EPS = 1e-6
NB = 256
BLOCKS = [(0, 128)] + [(128 + NB * i, NB) for i in range(TC // NB)]
G_AB, G_AC, G_AX, G_AG, G_BP, G_BG, G_CA, G_CGL, G_CG, G_DU, G_DV, G_DG = range(12)
PV_NG, PV_BADA, PV_WCA, PV_INVW, PV_PSC, PV_WDW, PV_BDW, PV_LGC, PV_LBC, PV_BPW, PV_FG, PV_C = \
    0, 8, 32, 38, 40, 42, 104, 106, 108, 110, 112, 120
PV_LGD, PV_LBD = 128, 130
NPV = 136


class TT:
    __slots__ = ("name", "w", "r")

    def __init__(self, name):
        self.name = name
        self.w = None
        self.r = []


class Op:
    __slots__ = ("eng", "fn", "dma", "key", "deps", "needs_inc", "val", "sem")


class Sched:
    ENGS = ("pe", "act", "dve", "pool", "sp")

    def __init__(self):
        self.ops = []

    def add(self, eng, fn, reads=(), writes=(), dma=False, key=None):
        op = Op()
        op.eng, op.fn, op.dma, op.key = eng, fn, dma, key
        op.needs_inc = False
        op.val = 0
        op.sem = None
        deps = {}
        for t in reads:
            if t.w is not None:
                deps.setdefault(id(t.w), [t.w, False])[1] = True
        for t in writes:
            if t.w is not None:
                deps.setdefault(id(t.w), [t.w, False])
            for r in t.r:
                deps.setdefault(id(r), [r, False])
        keep = []
        for d, raw in deps.values():
            if d is op:
                continue
            if d.dma or dma or d.eng != eng:
                keep.append(d)
            elif eng != "pe":
                keep.append(d)
        op.deps = keep
        for d in keep:
            if not d.dma:
                d.needs_inc = True
        for t in writes:
            t.w = op
            t.r = []
        for t in reads:
            if t.w is not op:
                t.r.append(op)
        self.ops.append(op)
        return op

    def emit(self, nc, es):
        esem = {e: es.enter_context(nc.semaphore("sem_" + e)) for e in self.ENGS}
        cnt = {e: 0 for e in self.ENGS}
        ksem = {}
        kcnt = {}
        for op in self.ops:
            if op.dma:
                k = id(op.key)
                if k not in ksem:
                    ksem[k] = es.enter_context(nc.semaphore("dsem%d" % len(ksem)))
                    kcnt[k] = 0
                kcnt[k] += 16
                op.sem = ksem[k]
                op.val = kcnt[k]
            else:
                if op.needs_inc:
                    cnt[op.eng] += 1
                op.sem = esem[op.eng]
                op.val = cnt[op.eng]
        ops = self.ops

        def body(e):
            def f(eng):
                waited = {}
                for op in ops:
                    if op.eng != e:
                        continue
                    for d in op.deps:
                        k = id(d.sem)
                        if waited.get(k, 0) < d.val:
                            eng.wait_ge(d.sem, d.val)
                            waited[k] = d.val
                    if op.fn is None:
                        continue
                    ins = op.fn(eng)
                    if op.dma:
                        ins.then_inc(op.sem, 16)
                    elif op.needs_inc:
                        ins.then_inc(op.sem, 1)
            return f

        with nc.Block() as block:
            block.tensor(body("pe"))
            block.scalar(body("act"))
            block.vector(body("dve"))
            block.gpsimd(body("pool"))
            block.sync(body("sp"))


def build_program(depth=DEPTH, dbg=False):
    nc = bass.Bass("TRN2", target_bir_lowering=False)
    es = ExitStack()
    S = Sched()

    def dram(name, shape, kind="ExternalInput"):
        return nc.dram_tensor(name, list(shape), F32, kind=kind).ap()

    xT_d = dram("xT", [DM, NT])
    pv_d = dram("pv", [DEPTH, 128, NPV])
    bc_d = dram("bc", [DEPTH, 128, 256])
    mats_d = dram("mats", [DEPTH, 128, 1408])
    win_d = dram("w_in", [DEPTH, DM, DIN])
    wout_d = dram("w_out", [DEPTH, DM, DM])
    wada_d = dram("w_ada", [DEPTH, DM, DIN])
    cst_d = dram("cst", [128, 128 + 32 + 1])
    out_d = dram("outT", [DM, TC], kind="ExternalOutput")

    def sb(name, shape, dt=F32):
        return es.enter_context(nc.sbuf_tensor(name, list(shape), dt))

    def ps(name, shape):
        return es.enter_context(nc.psum_tensor(name, list(shape), F32))

    xres = sb("xres", [128, KC, NT])
    win = sb("win", [128, KC, DIN], BF16)
    wout = sb("wout", [128, KC, DM], BF16)
    ht = [sb("ht%d" % i, [128, KC, NB], BF16) for i in range(2)]
    yt = sb("yt", [128, KC, NB], BF16)
    stg = sb("stg", [128, KC, NB], BF16)
    rstd = sb("rstd", [128, NB])
    rstdf = sb("rstdf", [128, NB])
    th = sb("th", [128, 2, NB], BF16)
    hc = sb("hc", [128, 2, 32 + NB], BF16)
    sgc = sb("sgc", [128, 2, NB], BF16)
    cv = sb("cv", [128, 2, NB])
    cvb = sb("cvb", [128, 2, NB], BF16)
    cvsq = sb("cvsq", [128, 2, NB], BF16)
    msq = sb("msq", [128, NB])
    rstdc = sb("rstdc", [128, NB])
    hs = cvsq
    gv = sb("gv", [128, 2, 256])
    mv = sb("mv", [128, 2, 8])
    mv2 = sb("mv2", [128, 2, 2])
    rsd = sb("rsd", [128, 2])
    vt = sb("vt", [128, 2, 256], BF16)
    sa1 = sb("sa1", [128, 2, NB], BF16)
    sa2 = sb("sa2", [128, 2, NB], BF16)
    sa3 = sb("sa3", [128, 2, NB], BF16)
    sa4 = sb("sa4", [128, 2, NB], BF16)
    ua = sb("ua", [128, 2, 2 + NB + 6], BF16)
    pb = sb("pb", [128, 2, 16 + NB])
    sgb = sb("sgb", [128, 2, NB], BF16)
    sA = sb("sA", [128, 2, 16 + NB])
    sB = sb("sB", [128, 2, 16 + NB])
    pld = sb("pld", [128, 2, NB], BF16)
    pv = sb("pvs", [128, DEPTH, NPV])
    bcs2 = [sb("bcs%d" % i, [128, 256]) for i in range(2)]
    bst2 = sb("bst2", [128, 2, 128])
    bzr_hi = sb("bzr_hi", [128, 256], BF16)
    bzr_lo = sb("bzr_lo", [128, 256], BF16)
    e0 = sb("e0", [128, 128], BF16)
    matsb2 = [sb("matsb%d" % i, [128, 1408], BF16) for i in range(2)]
    wst = sb("wst", [128, 4, 128], BF16)
    dgc = sb("dgc", [128, 62, 128], BF16)
    dga = sb("dga", [128, 6, 128], BF16)
    wdwb = sb("wdwb", [128, 68], BF16)
    cst = sb("csts", [128, 161])
    identb = sb("identb", [128, 128], BF16)
    onesb = sb("onesb", [128, 128], BF16)
    ones256 = sb("ones256", [128, 128], BF16)
    junk = sb("junk", [128, 128])
    cact = sb("cact", [128, KC], BF16)
    mod = sb("mod", [128, DEPTH, 24])
    gs = sb("gs", [128, DEPTH, KC])
    shb = sb("shb", [128, DEPTH, KC], BF16)
    bz = sb("bz", [128, 24])
    hbz = sb("hbz", [128, 2])
    hbzca = sb("hbzca", [128, 2])
    epsc = sb("epsc", [128, 1])
    zp = [ps("zp%d" % i, [128, 2, NB]) for i in range(3)]
    pst = ps("pst", [128, 2, NB])
    misc = [ps("misc%d" % i, [128, 2, NB]) for i in range(2)]
    dvp = ps("dvp", [128, 2, 256])
    mixp = ps("mixp", [128, 2, NB])

    T = {}

    def tt(name):
        if name not in T:
            T[name] = TT(name)
        return T[name]

    XT = [tt("xt%d" % b) for b in range(len(BLOCKS))]
    WIN = [tt("win%d" % g) for g in range(12)]
    WOUT = [tt("wout%d" % g) for g in range(4)]
    HT = [[tt("ht%d_%d" % (p, k)) for k in range(KC)] for p in range(2)]
    ZP = [tt("zp%d" % i) for i in range(3)]
    MISC = [tt("misc%d" % i) for i in range(2)]
    state = {"zp": 0, "misc": 0}

    def next_zp():
        i = state["zp"]
        state["zp"] = (i + 1) % 3
        return zp[i], ZP[i]

    def next_misc():
        i = state["misc"]
        state["misc"] = (i + 1) % 2
        return misc[i], MISC[i]

    add = S.add
    mult, addop, sub, powop = ALU.mult, ALU.add, ALU.subtract, ALU.pow

    add("sp", lambda e: e.dma_start(out=pv[:], in_=pv_d.rearrange("l p c -> p l c")),
        writes=[tt("pv")], dma=True, key=tt("pv"))
    add("sp", lambda e: e.dma_start(out=cst[:], in_=cst_d), writes=[tt("cst")], dma=True, key=tt("cst"))
    xT_v = xT_d.rearrange("(k p) t -> p k t", p=128)
    def load_x(b, after=()):
        t0, n = BLOCKS[b]
        add("sp", lambda e: e.dma_start(out=xres[:, :, t0:t0 + n], in_=xT_v[:, :, t0:t0 + n]),
            reads=list(after), writes=[XT[b]], dma=True, key=XT[b])
    load_x(0)
    load_x(1)
    add("dve", lambda e: e.memset(onesb[:], 1.0), writes=[tt("onesb")])
    add("dve", lambda e: e.memset(ones256[:], 1.0 / 256.0), writes=[tt("ones256")])
    add("dve", lambda e: e.memset(e0[:], 0.0), writes=[tt("e0")])
    add("dve", lambda e: e.memset(e0[0:1, :], 1.0), writes=[tt("e0")])
    add("dve", lambda e: e.memset(bzr_hi[:], 0.0), writes=[tt("bzr_hi")])
    add("dve", lambda e: e.memset(bzr_lo[:], 0.0), writes=[tt("bzr_lo")])
    add("dve", lambda e: e.tensor_copy(identb[:], cst[:, 0:128]), reads=[tt("cst")], writes=[tt("identb")])
    add("dve", lambda e: e.memset(hc[:], 0.0), writes=[tt("hc")])
    add("dve", lambda e: e.memset(ua[:], 0.0), writes=[tt("ua")])
    add("dve", lambda e: e.memset(pb[:], 0.0), writes=[tt("pb")])
    add("dve", lambda e: e.memset(sA[:], 0.0), writes=[tt("sA")])
    add("dve", lambda e: e.memset(sB[:], 0.0), writes=[tt("sB")])
    add("act", lambda e: e.activation(cact[:], pv[:, 0, PV_C:PV_C + 8], AF.Silu),
        reads=[tt("pv")], writes=[tt("cact")])
    add("dve", lambda e: e.memset(epsc[:], float(EPS)), writes=[tt("epsc")])

    def load_w(dsrc, l, g):
        src = dsrc[l].rearrange("(k p) c -> p k c", p=128)[:, :, g * 256:(g + 1) * 256]
        add("pool", lambda e: e.dma_start(out=win[:, :, g * 256:(g + 1) * 256], in_=src),
            writes=[WIN[g]], dma=True, key=WIN[g])

    def load_wout(l, g):
        src = wout_d[l].rearrange("(k p) c -> p k c", p=128)[:, :, g * 256:(g + 1) * 256]
        add("pool", lambda e: e.dma_start(out=wout[:, :, g * 256:(g + 1) * 256], in_=src),
            writes=[WOUT[g]], dma=True, key=WOUT[g])

    YT_ALL = [tt("yt0"), tt("yt2"), tt("yt4"), tt("yt6")]
    loaded_w = set()
    loaded_wout = set()

    def ada_piece(l, q, use_yt=False, wslot=None, part="both"):
        if wslot is not None:
            buf, BT = wout[:, :, wslot * 256:(wslot + 1) * 256], [WOUT[wslot]]
        elif use_yt:
            buf, BT = yt[:], YT_ALL
        else:
            buf, BT = stg[:], [tt("stg")]
        src = wada_d[l].rearrange("(k p) c -> p k c", p=128)[:, :, q * 256:(q + 1) * 256]
        if part in ("both", "issue"):
            add("pool", lambda e: e.dma_start(out=buf, in_=src), writes=BT, dma=True, key=BT[0])
        if part == "issue":
            return
        zt, ZT = next_zp()
        zrow = zt.rearrange("p a b -> p (a b)")

        def fn(e):
            ins = None
            for kc in range(KC):
                ins = e.matmul(zrow[:, 0:256], cact[:, kc:kc + 1].broadcast_to([128, 128]), buf[:, kc, :],
                               start=(kc == 0), stop=(kc == KC - 1))
            return ins
        add("pe", fn, reads=[tt("cact")] + BT, writes=[ZT])
        for m in range(2):
            add("dve", (lambda m=m: lambda e: e.scalar_tensor_tensor(
                junk[:, 0:128], zrow[:, m * 128:(m + 1) * 128], pv[:, l, PV_BADA + 2 * q + m:PV_BADA + 2 * q + m + 1],
                cst[:, 0:128], addop, mult, accum_out=mod[:, l, 2 * q + m:2 * q + m + 1]))(),
                reads=[ZT, tt("pv"), tt("cst")], writes=[tt("junk"), tt("mod%d_%d" % (l, q // 4))])

    def ada_finish_scale(l):
        add("dve", lambda e: e.scalar_tensor_tensor(gs[:, l, :], mod[:, l, 8:16], 1.0,
                                                    pv[:, l, PV_NG:PV_NG + 8], addop, mult),
            reads=[tt("mod%d_1" % l), tt("pv")], writes=[tt("gs%d" % l)])

    def ada_finish_shift(l):
        add("dve", lambda e: e.tensor_copy(shb[:, l, :], mod[:, l, 0:8]),
            reads=[tt("mod%d_0" % l)], writes=[tt("shb%d" % l)])

    def early_consts():
        l = 0
        bcs, matsb = bcs2[0], matsb2[0]
        add("sp", lambda e: e.dma_start(out=bcs[:], in_=bc_d[l]), writes=[tt("bcs0")], dma=True, key=tt("bcs0"))
        add("pool", lambda e: e.dma_start(out=matsb[:], in_=mats_d[l]), writes=[tt("matsb0")], dma=True, key=tt("matsb0"))

    def preload_w(g):
        load_w(win_d, 0, g)
        loaded_w.add(g)

    FIRST_USE = [G_CGL, G_CA, G_CG, G_BP, G_BG, G_DV, G_DU, G_DG, G_AC, G_AX, G_AG, G_AB]

    def startup_seq():
        early_consts()
        load_layer_consts(0)
        ada_piece(0, 4, use_yt=False)
        ada_piece(0, 5, use_yt=True)
        preload_w(FIRST_USE[0])
        ada_piece(0, 6, use_yt=False)
        preload_w(FIRST_USE[1])
        ada_piece(0, 7, use_yt=True)
        ada_finish_scale(0)
        pt0, SP0, _ = rms_sq(0, 0)
        rms_sqrt(pt0, SP0, rstd, "rstd", BLOCKS[0][1])
        rms_apply(0, 0, 0)
        load_x(2, after=[tt("gs0")])
        preload_w(FIRST_USE[2])
        ada_piece(0, 0, use_yt=False)
        preload_w(FIRST_USE[3])
        ada_piece(0, 1, use_yt=True)
        preload_w(FIRST_USE[4])
        ada_piece(0, 2, use_yt=False)
        preload_w(FIRST_USE[5])
        ada_piece(0, 3, use_yt=True)
        ada_finish_shift(0)
        preload_w(FIRST_USE[6])
        for g_ in FIRST_USE[7:]:
            preload_w(g_)
        for q_ in (8, 9, 10, 11):
            ada_piece(0, q_, wslot=q_ - 8, part="issue")
    ada_todo = [(1, q) for q in (4, 5, 6, 7, 0, 1, 2, 3, 8, 9, 10, 11)]

    def ada_next(cnt=1):
        for _ in range(cnt):
            if not ada_todo:
                return
            l_, q_ = ada_todo.pop(0)
            ada_piece(l_, q_)
            if l_ == 1 and q_ == 7:
                ada_finish_scale(1)
            if l_ == 1 and q_ == 3:
                ada_finish_shift(1)

    def load_layer_consts_dma(l):
        bcs, matsb = bcs2[l % 2], matsb2[l % 2]
        add("sp", lambda e: e.dma_start(out=bcs[:], in_=bc_d[l]), writes=[tt("bcs%d" % (l % 2))], dma=True, key=tt("bcs%d" % (l % 2)))
        add("pool", lambda e: e.dma_start(out=matsb[:], in_=mats_d[l]), writes=[tt("matsb%d" % (l % 2))], dma=True,
            key=tt("matsb%d" % (l % 2)))

    def build_dgc(l):
        add("dve", lambda e: e.tensor_copy(wdwb[:, 0:62], pv[:, l, PV_WDW:PV_WDW + 62]),
            reads=[tt("pv")], writes=[tt("wdwb")])
        add("dve", lambda e: e.tensor_tensor(dgc[:], identb[:].unsqueeze(1).broadcast_to([128, 62, 128]),
                                              wdwb[:, 0:62].unsqueeze(2).broadcast_to([128, 62, 128]), mult),
            reads=[tt("identb"), tt("wdwb")], writes=[tt("dgc")])

    def load_layer_consts(l):
        bcs, matsb = bcs2[l % 2], matsb2[l % 2]
        bst_v = bcs[:, 0:256].rearrange("p (j s) -> p j s", j=2)
        add("dve", lambda e: e.tensor_tensor(wst[:], matsb[:, 0:512].rearrange("p (h s) -> p h s", h=4),
                                             matsb[:, 512:640].unsqueeze(1).broadcast_to([128, 4, 128]), mult),
            reads=[tt("matsb%d" % (l % 2))], writes=[tt("wst")])
        rsp, RSP = next_zp()
        rsp2 = rsp.rearrange("p a b -> p (a b)")
        add("pe", lambda e: e.matmul(rsp2[:, 0:512], onesb[:], wst[:].rearrange("p h s -> p (h s)"), start=True, stop=True),
            reads=[tt("wst"), tt("onesb")], writes=[RSP])
        for j in range(2):
            for hl in range(2):
                h = 2 * j + hl
                add("dve", (lambda j=j, hl=hl, h=h: lambda e: e.scalar_tensor_tensor(
                    bst2[hl * 64:(hl + 1) * 64, j, :], rsp2[hl * 64:(hl + 1) * 64, h * 128:(h + 1) * 128],
                    pv[hl * 64:(hl + 1) * 64, l, PV_LBD + j:PV_LBD + j + 1], bst_v[hl * 64:(hl + 1) * 64, j, :],
                    mult, addop))(),
                    reads=[RSP, tt("pv"), tt("bcs%d" % (l % 2))], writes=[tt("bst2")])
        if l == 0:
            build_dgc(l)
        add("dve", lambda e: e.tensor_copy(wdwb[:, 62:68], pv[:, l, PV_WCA:PV_WCA + 6]),
            reads=[tt("pv")], writes=[tt("wdwa")])
        add("dve", lambda e: e.tensor_tensor(dga[:], identb[:].unsqueeze(1).broadcast_to([128, 6, 128]),
                                              wdwb[:, 62:68].unsqueeze(2).broadcast_to([128, 6, 128]), mult),
            reads=[tt("identb"), tt("wdwa")], writes=[tt("dga")])

    wpool_vs = [m_[:, 640:896].rearrange("p (j m) -> p j m", j=2) for m_ in matsb2]
    wpw2_vs = [m_[:, 896:1408].rearrange("p (k m) -> p k m", k=2) for m_ in matsb2]

    def bias_group(l, g):
        zt, ZT = next_zp()
        zrow = zt.rearrange("p a b -> p (a b)")

        def fn(e):
            ins = None
            for kc in range(KC):
                ins = e.matmul(zrow[:, 0:256], shb[:, l, kc:kc + 1].broadcast_to([128, 128]),
                               win[:, kc, g * 256:(g + 1) * 256], start=(kc == 0), stop=(kc == KC - 1))
            return ins
        add("pe", fn, reads=[tt("shb%d" % l), WIN[g]], writes=[ZT])
        if g == G_DV:
            add("dve", lambda e: e.tensor_copy(bzr_hi[0:1, :], zrow[0:1, 0:256]), reads=[ZT], writes=[tt("bzr_hi")])
            add("dve", lambda e: e.tensor_tensor(rstd[0:1, 0:256], zrow[0:1, 0:256], bzr_hi[0:1, :], sub),
                reads=[ZT, tt("bzr_hi")], writes=[tt("rstd")])
            add("dve", lambda e: e.tensor_copy(bzr_lo[0:1, :], rstd[0:1, 0:256]), reads=[tt("rstd")], writes=[tt("bzr_lo")])
        else:
            for m in range(2):
                add("dve", (lambda m=m: lambda e: e.scalar_tensor_tensor(
                    junk[:, 0:128], zrow[:, m * 128:(m + 1) * 128], 1.0, cst[:, 0:128], mult, mult,
                    accum_out=bz[:, 2 * g + m:2 * g + m + 1]))(),
                    reads=[ZT, tt("cst")], writes=[tt("junk"), tt("bz%d" % g)])
            if g == G_CGL:
                add("dve", lambda e: e.tensor_scalar(hbz[:], bz[:, 2 * g:2 * g + 2], 0.5, None, mult),
                    reads=[tt("bz%d" % g)], writes=[tt("hbz")])
            if g == G_CA:
                add("dve", lambda e: e.tensor_scalar(hbzca[:], bz[:, 2 * g:2 * g + 2], 0.5, None, mult),
                    reads=[tt("bz%d" % g)], writes=[tt("hbzca")])

    stores = []
    NBLK = len(BLOCKS)
    seq = [(l, b) for l in range(depth) for b in range(NBLK)]

    def rms_sq(b, par, bank=None, half=0, act_only=False, skip_act=False):
        t0, n = BLOCKS[b]
        htp = ht[par]
        if not skip_act:
            add("act", lambda e: e.activation(htp[:, :, 0:n], xres[:, :, t0:t0 + n], AF.Square),
                reads=[XT[b]], writes=HT[par])
        if act_only:
            return None
        if bank is None:
            sp_, SP_ = next_misc()
        else:
            sp_, SP_ = bank
        pt = sp_[:, half, :]

        def fn(e):
            ins = None
            for kc in range(KC):
                ins = e.matmul(pt[:, 0:n], onesb[:], htp[:, kc, 0:n], start=(kc == 0), stop=(kc == KC - 1))
            return ins
        add("pe", fn, reads=HT[par] + [tt("onesb")], writes=[SP_])
        return pt, SP_, sp_

    def rms_sqrt(pt, SP_, dst, dname, n):
        add("act", lambda e: e.activation(dst[:, 0:n], pt[:, 0:n], AF.Ln, bias=epsc[:, 0:1], scale=1.0 / DM),
            reads=[SP_, tt("epsc")], writes=[tt(dname)])
        add("act", lambda e: e.activation(dst[:, 0:n], dst[:, 0:n], AF.Exp, scale=-0.5),
            reads=[tt(dname)], writes=[tt(dname)])

    def rms_apply(l, b, par, kcs=range(KC)):
        t0, n = BLOCKS[b]
        htp = ht[par]
        for kc in kcs:
            add("dve", (lambda kc=kc: lambda e: e.scalar_tensor_tensor(
                htp[:, kc, 0:n], xres[:, kc, t0:t0 + n], gs[:, l, kc:kc + 1], rstd[:, 0:n], mult, mult))(),
                reads=[XT[b], tt("rstd"), tt("gs%d" % l)], writes=[HT[par][kc]])

    def fin_apply_store(b):
        t0, n = BLOCKS[b]
        for kc in range(KC):
            add("dve", (lambda kc=kc: lambda e: e.scalar_tensor_tensor(
                xres[:, kc, t0:t0 + n], xres[:, kc, t0:t0 + n], pv[:, 0, PV_FG + kc:PV_FG + kc + 1], rstdf[:, 0:n],
                mult, mult))(),
                reads=[XT[b], tt("rstdf"), tt("pv")], writes=[XT[b]])
        ov = out_d.rearrange("(k p) t -> p k t", p=128)[:, :, t0 - HALO:t0 - HALO + n]
        st = tt("store%d" % b)
        add("sp", lambda e: e.dma_start(out=ov, in_=xres[:, :, t0:t0 + n]),
            reads=[XT[b]], writes=[st], dma=True, key=st)
        stores.append(st)

    MIXB = (mixp, tt("mixp"))

    def pipe_next(i):
        if i + 1 >= len(seq):
            return False
        l, b = seq[i]
        if seq[i + 1][0] != l:
            return False
        if l == depth - 1 and b == 0:
            return False
        return True

    def block(i, mode="full"):
        l, b = seq[i]
        t0, n = BLOCKS[b]
        par = i % 2
        htp = ht[par]
        ns = n // 128
        first = (b == 0)
        last_layer = (l == depth - 1)
        last_blk = (b == NBLK - 1)
        nxt = seq[i + 1] if i + 1 < len(seq) else None
        reload_w = last_blk and (l + 1 < depth)
        do_fin = last_layer and b >= 2
        hreads = list(HT[par])
        W = 16 + n
        wpool_v, wpw2_v = wpool_vs[l % 2], wpw2_vs[l % 2]
        st8 = {}

        def ada_pt(k):
            if l != 0 or last_blk:
                return
            if first:
                if k == 0:
                    for q_ in (8, 9, 10, 11):
                        ada_piece(0, q_, wslot=q_ - 8, part="process")
                    for g_ in range(4):
                        load_wout(0, g_)
                        loaded_wout.add(g_)
            elif (k == 0) or (k == 1 and b <= 5):
                ada_next(1)

        def inproj(g):
            if first:
                if l == 0 and g not in loaded_w:
                    load_w(win_d, 0, g)
                bias_group(l, g)
            zt, ZT = next_zp()

            def fn(e):
                ins = None
                for j in range(2):
                    for kc in range(KC):
                        ins = e.matmul(zt[:, j, 0:n], win[:, kc, (2 * g + j) * 128:(2 * g + j + 1) * 128],
                                       htp[:, kc, 0:n], start=(kc == 0), stop=(kc == KC - 1))
                return ins
            add("pe", fn, reads=hreads + [WIN[g]], writes=[ZT])
            if reload_w:
                load_w(win_d, l + 1, g)
            return zt, ZT

        def prefetch_table(func, slot):
            dst = (mv[:, 0, 6:7], mv[:, 0, 7:8], mv[:, 1, 6:7])[slot]
            add("act", lambda e: e.activation(dst, epsc[:, 0:1], func), reads=[tt("epsc")], writes=[tt("dum%d" % slot)])

        def act2(zt, ZT, dst, dname, func, g, col=0, scale=1.0, biast=None, extra_reads=()):
            for j in range(2):
                bias_ap = biast[:, j:j + 1] if biast is not None else bz[:, 2 * g + j:2 * g + j + 1]
                add("act", (lambda j=j, bias_ap=bias_ap: lambda e: e.activation(
                    dst[:, j, col:col + n], zt[:, j, 0:n], func, bias=bias_ap, scale=scale))(),
                    reads=[ZT, tt("bz%d" % g)] + list(extra_reads), writes=[tt(dname)])

        def S_glu():
            zt, ZT = inproj(G_CGL)
            act2(zt, ZT, th, "th", AF.Tanh, G_CGL, scale=0.5, biast=hbz, extra_reads=[tt("hbz")])
            zt, ZT = inproj(G_CA)
            act2(zt, ZT, sa3, "sa3", AF.Identity, G_CA, scale=0.5, biast=hbzca, extra_reads=[tt("hbzca")])
            add("dve", lambda e: e.scalar_tensor_tensor(hc[:, :, 32:32 + n], th[:, :, 0:n], 1.0, sa3[:, :, 0:n], addop, mult),
                reads=[tt("th"), tt("sa3")], writes=[tt("hc")])

        def S_ssnxt():
            if nxt is not None:
                st8["nx"] = rms_sq(nxt[1], (i + 1) % 2, bank=MIXB, half=0, skip_act=True)
            if do_fin:
                rms_sq(b - 1, (i - 1) % 2, act_only=True)

        def S_cg():
            zt, ZT = inproj(G_CG)
            act2(zt, ZT, sgc, "sgc", AF.Silu, G_CG)

        def S_c31():
            cvp, CVP = next_misc()

            def fn_c31(e):
                ins = None
                for j in range(2):
                    for k in range(31):
                        ins = e.matmul(cvp[:, j, 0:n], dgc[:, j * 31 + k, :], hc[:, j, 2 + k:2 + k + n],
                                       start=(k == 0), stop=(k == 30))
                return ins
            add("pe", fn_c31, reads=[tt("hc"), tt("dgc")], writes=[CVP])
            for j in range(2):
                add("act", (lambda j=j: lambda e: e.activation(cvb[:, j, 0:n], cvp[:, j, 0:n], AF.Identity,
                                                               bias=pv[:, l, PV_BDW + j:PV_BDW + j + 1]))(),
                    reads=[CVP, tt("pv")], writes=[tt("cvb")])
                add("act", (lambda j=j: lambda e: e.activation(cvsq[:, j, 0:n], cvp[:, j, 0:n], AF.Square,
                                                               bias=pv[:, l, PV_BDW + j:PV_BDW + j + 1]))(),
                    reads=[CVP, tt("pv")], writes=[tt("cvsq")])

        def S_dv():
            if first:
                if l == 0 and G_DV not in loaded_w:
                    load_w(win_d, 0, G_DV)
                bias_group(l, G_DV)

            def fn_dv(e):
                ins = None
                for s in range(ns):
                    for kc in range(KC):
                        ins = e.matmul(dvp[:, s, :], htp[:, kc, s * 128:(s + 1) * 128],
                                       win[:, kc, G_DV * 256:(G_DV + 1) * 256], start=(kc == 0), stop=False)
                    e.matmul(dvp[:, s, :], e0[:], bzr_hi[:], start=False, stop=False)
                    ins = e.matmul(dvp[:, s, :], e0[:], bzr_lo[:], start=False, stop=True)
                return ins
            add("pe", fn_dv, reads=hreads + [WIN[G_DV], tt("bzr_hi"), tt("bzr_lo"), tt("e0")], writes=[tt("dvp")])
            if reload_w:
                load_w(win_d, l + 1, G_DV)
            add("act", lambda e: e.activation(gv[:, 0:ns, :], dvp[:, 0:ns, :], AF.Gelu_apprx_tanh),
                reads=[tt("dvp")], writes=[tt("gv")])
            for s in range(ns):
                add("dve", (lambda s=s: lambda e: e.bn_stats(mv[:, s, 0:6], gv[:, s, :]))(),
                    reads=[tt("gv")], writes=[tt("mv")])
            for s in range(ns):
                add("dve", (lambda s=s: lambda e: e.bn_aggr(mv2[:, s, :], mv[:, s, 0:6]))(),
                    reads=[tt("mv")], writes=[tt("mv2")])

        def S_du():
            zt, ZT = inproj(G_DU)
            act2(zt, ZT, sa1, "sa1", AF.Gelu_apprx_tanh, G_DU)
            prefetch_table(AF.Ln, 1)

        def S_stats():
            def fn_st(e):
                ins = None
                for j in range(2):
                    ins = e.matmul(pst[:, 0, 0:n], ones256[:], cvb[:, j, 0:n], start=(j == 0), stop=(j == 1))
                for j in range(2):
                    ins = e.matmul(pst[:, 1, 0:n], ones256[:], cvsq[:, j, 0:n], start=(j == 0), stop=(j == 1))
                return ins
            add("pe", fn_st, reads=[tt("cvb"), tt("cvsq"), tt("ones256")], writes=[tt("pst")])
            add("act", lambda e: e.activation(msq[:, 0:n], pst[:, 0, 0:n], AF.Square), reads=[tt("pst")], writes=[tt("msq")])
            add("dve", lambda e: e.scalar_tensor_tensor(rstdc[:, 0:n], pst[:, 1, 0:n], float(EPS), msq[:, 0:n], addop, sub),
                reads=[tt("pst"), tt("msq")], writes=[tt("rstdc")])

        def S_b():
            zt, ZT = inproj(G_BP)
            act2(zt, ZT, pb, "pb", AF.Identity, G_BP, col=16)
            zt, ZT = inproj(G_BG)
            act2(zt, ZT, sgb, "sgb", AF.Silu, G_BG)
            prefetch_table(AF.Gelu_apprx_tanh, 0)
            add("dve", lambda e: e.tensor_tensor(sA[:, :, 2:W], pb[:, :, 2:W], pb[:, :, 1:W - 1], addop),
                reads=[tt("pb")], writes=[tt("sA")])
            add("dve", lambda e: e.tensor_tensor(sB[:, :, 4:W], sA[:, :, 4:W], sA[:, :, 2:W - 2], addop),
                reads=[tt("sA")], writes=[tt("sB")])
            add("dve", lambda e: e.tensor_tensor(sA[:, 1, 8:W], sB[:, 1, 8:W], sB[:, 1, 4:W - 4], addop),
                reads=[tt("sB"), tt("sA")], writes=[tt("sA")])
            add("dve", lambda e: e.tensor_tensor(sB[64:128, 1, 16:W], sA[64:128, 1, 16:W], sA[64:128, 1, 8:W - 8], addop),
                reads=[tt("sA"), tt("sB")], writes=[tt("sB")])
            if b == 1:
                cf = cst[:, 128:160].rearrange("p (j t) -> p j t", j=2)
                add("dve", lambda e: e.tensor_tensor(sA[:, :, 16:32], sA[:, :, 16:32], cf, mult),
                    reads=[tt("sA"), tt("cst")], writes=[tt("sA")])
                add("dve", lambda e: e.tensor_tensor(sB[:, :, 16:32], sB[:, :, 16:32], cf, mult),
                    reads=[tt("sB"), tt("cst")], writes=[tt("sB")])
            for j in range(2):
                for hf, src, srct in ((0, sA, "sA"), (1, sB, "sB")):
                    p0, p1 = hf * 64, hf * 64 + 64
                    add("dve", (lambda j=j, p0=p0, p1=p1, src=src: lambda e: e.scalar_tensor_tensor(
                        pld[p0:p1, j, 0:n], src[p0:p1, j, 16:W], pv[p0:p1, l, PV_INVW + j:PV_INVW + j + 1],
                        pb[p0:p1, j, 16:W], mult, sub))(),
                        reads=[tt(srct), tt("pb"), tt("pv")], writes=[tt("pld")])

        def S_finss():
            if do_fin:
                st8["fin"] = rms_sq(b - 1, (i - 1) % 2, bank=MIXB, half=1, skip_act=True)

        def S_burst():
            add("act", lambda e: e.activation(rstdc[:, 0:n], rstdc[:, 0:n], AF.Ln),
                reads=[tt("rstdc")], writes=[tt("rstdc")])
            add("act", lambda e: e.activation(rsd[:, 0:ns], mv2[:, 0:ns, 1], AF.Ln, bias=epsc[:, 0:1]),
                reads=[tt("mv2"), tt("epsc")], writes=[tt("rsd")])
            add("act", lambda e: e.activation(rstdc[:, 0:n], rstdc[:, 0:n], AF.Exp, scale=-0.5),
                reads=[tt("rstdc")], writes=[tt("rstdc")])
            add("act", lambda e: e.activation(rsd[:, 0:ns], rsd[:, 0:ns], AF.Exp, scale=-0.5),
                reads=[tt("rsd")], writes=[tt("rsd")])
            if "nx" in st8:
                rms_sqrt(st8["nx"][0], st8["nx"][1], rstd, "rstd", BLOCKS[nxt[1]][1])
            if "fin" in st8:
                rms_sqrt(st8["fin"][0], st8["fin"][1], rstdf, "rstdf", BLOCKS[b - 1][1])
            prefetch_table(AF.Silu, 2)
            add("dve", lambda e: e.tensor_tensor(cv[:, :, 0:n], cvb[:, :, 0:n],
                                                 pst[:, 0:1, 0:n].broadcast_to([128, 2, n]), sub),
                reads=[tt("cvb"), tt("pst")], writes=[tt("cv")])
            add("dve", lambda e: e.tensor_tensor(cv[:, :, 0:n], cv[:, :, 0:n],
                                                 rstdc[:, 0:n].unsqueeze(1).broadcast_to([128, 2, n]), mult),
                reads=[tt("cv"), tt("rstdc")], writes=[tt("cv")])
            for s in range(ns):
                add("dve", (lambda s=s: lambda e: e.tensor_scalar(vt[:, s, :], gv[:, s, :], mv2[:, s, 0:1],
                                                                  rsd[:, s:s + 1], sub, mult))(),
                    reads=[tt("gv"), tt("mv2"), tt("rsd")], writes=[tt("vt")])
            for j in range(2):
                add("act", (lambda j=j: lambda e: e.activation(hs[:, j, 0:n], cv[:, j, 0:n], AF.Silu,
                                                               bias=pv[:, l, PV_LBC + j:PV_LBC + j + 1],
                                                               scale=pv[:, l, PV_LGC + j:PV_LGC + j + 1]))(),
                    reads=[tt("cv"), tt("pv")], writes=[tt("cvsq")])

        def S_dg():
            zt, ZT = inproj(G_DG)
            act2(zt, ZT, sa2, "sa2", AF.Silu, G_DG)
            add("dve", lambda e: e.tensor_tensor(sa1[:, :, 0:n], sa1[:, :, 0:n], sa2[:, :, 0:n], mult),
                reads=[tt("sa1"), tt("sa2")], writes=[tt("sa1")])

        def S_pool():
            plp, PLP = next_misc()

            def fn_pl(e):
                ins = None
                for j in range(2):
                    ins = e.matmul(plp[:, j, 0:n], wpool_v[:, j, :], pld[:, j, 0:n], start=True, stop=True)
                return ins
            add("pe", fn_pl, reads=[tt("pld"), tt("matsb%d" % (l % 2))], writes=[PLP])
            for j in range(2):
                add("dve", (lambda j=j: lambda e: e.scalar_tensor_tensor(
                    yt[:, 2 + j, 0:n], plp[:, j, 0:n], pv[:, l, PV_PSC + j:PV_PSC + j + 1], sgb[:, j, 0:n], mult, mult))(),
                    reads=[PLP, tt("pv"), tt("sgb")], writes=[tt("yt2")])

        def S_ac():
            zt, ZT = inproj(G_AC)
            act2(zt, ZT, sa3, "sa3", AF.Identity, G_AC)

        def S_sgu():
            def fn_sgu(e):
                ins = None
                for s in range(ns):
                    for h in range(4):
                        j, hl = h // 2, h % 2
                        ins = e.matmul(mixp[hl * 64:(hl + 1) * 64, j, s * 128:(s + 1) * 128],
                                       vt[:, s, h * 64:(h + 1) * 64], wst[:, h, :], start=True, stop=True)
                return ins
            add("pe", fn_sgu, reads=[tt("vt"), tt("wst")], writes=[tt("mixp")])
            for j in range(2):
                add("dve", (lambda j=j: lambda e: e.scalar_tensor_tensor(
                    gv[:, j, 0:n].rearrange("p (s q) -> p s q", q=128), mixp[:, j, 0:n].rearrange("p (s q) -> p s q", q=128),
                    pv[:, l, PV_LGD + j:PV_LGD + j + 1], bst2[:, j, :].unsqueeze(1).broadcast_to([128, ns, 128]),
                    mult, addop))(),
                    reads=[tt("mixp"), tt("bst2"), tt("pv")], writes=[tt("gv")])
            add("dve", lambda e: e.tensor_tensor(yt[:, 6:8, 0:n], gv[:, :, 0:n], sa1[:, :, 0:n], mult),
                reads=[tt("sa1"), tt("gv")], writes=[tt("yt6")])

        def S_ax():
            zt, ZT = inproj(G_AX)
            for j in range(2):
                add("dve", (lambda j=j, zt=zt: lambda e: e.scalar_tensor_tensor(
                    ua[:, j, 2:2 + n], zt[:, j, 0:n], bz[:, 2 * G_AX + j:2 * G_AX + j + 1], sa3[:, j, 0:n], addop, mult))(),
                    reads=[ZT, tt("bz%d" % G_AX), tt("sa3")], writes=[tt("ua")])

        def S_pw():
            pwp, PWP = next_misc()

            def fn_pw(e):
                ins = None
                for jo in range(2):
                    for k2 in range(2):
                        ins = e.matmul(pwp[:, jo, 0:n], wpw2_v[:, k2, jo * 128:(jo + 1) * 128], hs[:, k2, 0:n],
                                       start=(k2 == 0), stop=(k2 == 1))
                return ins
            add("pe", fn_pw, reads=[tt("cvsq"), tt("matsb%d" % (l % 2))], writes=[PWP])
            for j in range(2):
                add("dve", (lambda j=j: lambda e: e.scalar_tensor_tensor(
                    yt[:, 4 + j, 0:n], pwp[:, j, 0:n], pv[:, l, PV_BPW + j:PV_BPW + j + 1], sgc[:, j, 0:n], addop, mult))(),
                    reads=[PWP, tt("pv"), tt("sgc")], writes=[tt("yt4")])

        def S_ag_ab():
            zt, ZT = inproj(G_AG)
            act2(zt, ZT, sa4, "sa4", AF.Silu, G_AG)
            zt, ZT = inproj(G_AB)
            for j in range(2):
                add("dve", (lambda j=j, zt=zt: lambda e: e.scalar_tensor_tensor(
                    sa4[:, j, 0:n], zt[:, j, 0:n], bz[:, 2 * G_AB + j:2 * G_AB + j + 1], sa4[:, j, 0:n], addop, mult))(),
                    reads=[ZT, tt("bz%d" % G_AB), tt("sa4")], writes=[tt("sa4")])

        def S_c3():
            c3p, C3P = next_misc()

            def fn_c3(e):
                ins = None
                for j in range(2):
                    for k in range(3):
                        ins = e.matmul(c3p[:, j, 0:n], dga[:, j * 3 + k, :], ua[:, j, k:k + n],
                                       start=(k == 0), stop=(k == 2))
                return ins
            add("pe", fn_c3, reads=[tt("ua"), tt("dga")], writes=[C3P])
            add("dve", lambda e: e.tensor_tensor(yt[:, 0:2, 0:n], c3p[:, :, 0:n], sa4[:, :, 0:n], mult),
                reads=[C3P, tt("sa4")], writes=[tt("yt0")])

        def S_tail():
            if not last_blk:
                for buf, name, hw in ((ua, "ua", 2), (pb, "pb", 16), (hc, "hc", 32)):
                    if first:
                        add("pool", (lambda buf=buf, hw=hw: lambda e: e.tensor_scalar(
                            buf[:, :, 0:hw], buf[:, :, n:n + hw], cst[:, 160:161], None, mult))(),
                            reads=[tt(name), tt("cst")], writes=[tt(name)])
                    else:
                        add("pool", (lambda buf=buf, hw=hw: lambda e: e.tensor_copy(buf[:, :, 0:hw], buf[:, :, n:n + hw]))(),
                            reads=[tt(name)], writes=[tt(name)])
            if l == 0 and b + 3 < NBLK:
                load_x(b + 3, after=[tt("yt0")])

        def S_out(fo2s=range(4)):
            yreads = [tt("yt0"), tt("yt2"), tt("yt4"), tt("yt6")]
            for fo2 in fo2s:
                if first and l == 0 and fo2 not in loaded_wout:
                    load_wout(0, fo2)
                zt, ZT = next_zp()

                def fn_o(e, fo2=fo2, zt=zt):
                    ins = None
                    for j in range(2):
                        fo = 2 * fo2 + j
                        for kc in range(KC):
                            ins = e.matmul(zt[:, j, 0:n], wout[:, kc, fo * 128:(fo + 1) * 128], yt[:, kc, 0:n],
                                           start=(kc == 0), stop=(kc == KC - 1))
                    return ins
                add("pe", fn_o, reads=yreads + [WOUT[fo2]], writes=[ZT])
                if reload_w:
                    load_wout(l + 1, fo2)
                for j in range(2):
                    fo = 2 * fo2 + j
                    add("dve", (lambda j=j, fo=fo, zt=zt: lambda e: e.scalar_tensor_tensor(
                        xres[:, fo, t0:t0 + n], zt[:, j, 0:n], mod[:, l, 16 + fo:17 + fo], xres[:, fo, t0:t0 + n],
                        mult, addop))(),
                        reads=[ZT, tt("mod%d_2" % l), XT[b]], writes=[XT[b]])
                if nxt is not None and not pipe_next(i):
                    rms_apply(nxt[0], nxt[1], (i + 1) % 2, kcs=(2 * fo2, 2 * fo2 + 1))

        if mode == "head_a":
            S_glu()
            return
        if mode == "head_b":
            S_c31()
            return
        skip_head = (i > 0 and pipe_next(i - 1))
        pipe = pipe_next(i)
        if last_layer and first:
            rms_sq(nxt[1], (i + 1) % 2, act_only=True)
            S_glu()
            S_ssnxt()
            zt, ZT = inproj(G_BP)
            act2(zt, ZT, pb, "pb", AF.Identity, G_BP, col=16)
            rms_sqrt(st8["nx"][0], st8["nx"][1], rstd, "rstd", BLOCKS[nxt[1]][1])
            S_ac()
            S_ax()
            S_tail()
            rms_apply(nxt[0], nxt[1], (i + 1) % 2)
            for g_ in (G_CG, G_BG, G_DV, G_DU, G_DG, G_AG, G_AB):
                bias_group(l, g_)
            return
        if nxt is not None:
            rms_sq(nxt[1], (i + 1) % 2, act_only=True)
        if not skip_head:
            S_glu()
        S_ssnxt()
        S_cg()
        if not skip_head:
            S_c31()
        S_b()
        S_dv()
        ada_pt(0)
        S_du()
        S_stats()
        S_finss()
        S_burst()
        if last_blk and l + 1 < depth:
            assert skip_head
            build_dgc(l + 1)
        if pipe:
            rms_apply(nxt[0], nxt[1], (i + 1) % 2)
        S_dg()
        S_pool()
        S_ac()
        S_sgu()
        S_ax()
        S_pw()
        S_ag_ab()
        S_c3()
        ada_pt(1)
        S_tail()
        if pipe:
            block(i + 1, mode="head_a")
            S_out((0, 1))
            block(i + 1, mode="head_b")
            S_out((2, 3))
        else:
            S_out()
        if do_fin:
            fin_apply_store(b - 1)

    startup_seq()
    for i in range(len(seq)):
        if seq[i][1] == 0 and seq[i][0] > 0:
            load_layer_consts(seq[i][0])
        if seq[i] == (0, 2) and depth > 1:
            load_layer_consts_dma(1)
        block(i)
    add("act", lambda e: e.activation(mv[:, 1, 7:8], epsc[:, 0:1], AF.Ln), reads=[tt("epsc")], writes=[tt("dum3")])
    ptf, SPF, _ = rms_sq(NBLK - 1, (len(seq) - 1) % 2)
    rms_sqrt(ptf, SPF, rstdf, "rstdf", BLOCKS[NBLK - 1][1])
    fin_apply_store(NBLK - 1)
    add("sp", None, reads=stores)
    S.emit(nc, es)
    es.close()
    return nc


def _pack_inputs(inp):
    f = lambda a: np.ascontiguousarray(np.asarray(a, dtype=np.float32))
    x = f(inp["x"])[0]
    c = f(inp["c"])[0]
    col = lambda v: v.reshape(-1, 128).T
    pvs, bcs, matss = [], [], []
    tril_t = (np.arange(128)[:, None] <= np.arange(128)[None, :]).astype(np.float32)
    for l in range(DEPTH):
        pv = np.zeros((128, NPV), np.float32)
        pv[:, PV_NG:PV_NG + 8] = col(f(inp["norm_g"])[l])
        pv[:, PV_BADA:PV_BADA + 24] = col(f(inp["b_ada"])[l])
        wca = f(inp["w_conv_a"])[l]
        wdw = f(inp["w_dw_c"])[l]
        for j in range(2):
            pv[:, PV_WCA + 3 * j:PV_WCA + 3 * j + 3] = wca[:, j * 128:(j + 1) * 128].T
            pv[:, PV_WDW + 31 * j:PV_WDW + 31 * j + 31] = wdw[:, j * 128:(j + 1) * 128].T
        pv[0:64, PV_INVW] = 1.0 / 2
        pv[64:128, PV_INVW] = 1.0 / 4
        pv[0:64, PV_INVW + 1] = 1.0 / 8
        pv[64:128, PV_INVW + 1] = 1.0 / 16
        pv[:, PV_PSC:PV_PSC + 2] = col(f(inp["pool_scale"])[l])
        pv[:, PV_BDW:PV_BDW + 2] = col(f(inp["b_dw_c"])[l])
        pv[:, PV_LGC:PV_LGC + 2] = col(f(inp["ln_g_c"])[l])
        pv[:, PV_LBC:PV_LBC + 2] = col(f(inp["ln_b_c"])[l])
        pv[:, PV_BPW:PV_BPW + 2] = col(f(inp["b_pw2_c"])[l])
        pv[:, PV_FG:PV_FG + 8] = col(f(inp["final_g"]))
        pv[:, PV_C:PV_C + 8] = col(c)
        pv[:, PV_LGD:PV_LGD + 2] = col(f(inp["ln_g_d"])[l])
        pv[:, PV_LBD:PV_LBD + 2] = col(f(inp["ln_b_d"])[l])
        pvs.append(pv)
        bc = np.zeros((128, 256), np.float32)
        bs = f(inp["b_s_d"])[l]
        for j in range(2):
            for hl in range(2):
                bc[hl * 64:(hl + 1) * 64, j * 128:(j + 1) * 128] = bs[2 * j + hl][None, :]
        bcs.append(bc)
        m = np.zeros((128, 1408), np.float32)
        ws = f(inp["w_s_d"])[l]
        for h in range(4):
            m[:, h * 128:(h + 1) * 128] = ws[h].T
        m[:, 512:640] = tril_t
        wp = f(inp["w_pool"])[l]
        for j in range(2):
            for gl in range(2):
                m[gl * 64:(gl + 1) * 64, 640 + j * 128 + gl * 64:640 + j * 128 + (gl + 1) * 64] = wp[2 * j + gl]
        w2 = f(inp["w_pw2_c"])[l]
        for k in range(2):
            m[:, 896 + k * 256:896 + (k + 1) * 256] = w2[k * 128:(k + 1) * 128, :]
        matss.append(m)
    pv_a, bc_a, mats_a = np.stack(pvs), np.stack(bcs), np.stack(matss)
    w_in, w_out, w_ada = f(inp["w_in"]), f(inp["w_out"]), f(inp["w_ada"])
    in_maps = []
    winv = np.array([2, 4, 8, 16], np.float32)
    for core in range(NCORE):
        xt = np.zeros((DM, NT), np.float32)
        lo = core * TC - HALO
        if lo < 0:
            xt[:, HALO:] = x[0:TC].T
        else:
            xt[:, :] = x[lo:lo + NT].T
        cst = np.zeros((128, 161), np.float32)
        cst[:, 0:128] = np.eye(128, dtype=np.float32)
        cf = np.ones((128, 2, 16), np.float32)
        if core == 0:
            for j in range(2):
                for hf in range(2):
                    w = winv[2 * j + hf]
                    tpos = np.arange(16, dtype=np.float32)
                    cf[hf * 64:(hf + 1) * 64, j, :] = w / np.minimum(tpos + 1, w)
        cst[:, 128:160] = cf.reshape(128, 32)
        cst[:, 160] = 0.0 if core == 0 else 1.0
        in_maps.append({"xT": np.ascontiguousarray(xt), "pv": pv_a, "bc": bc_a, "mats": mats_a,
                        "w_in": w_in, "w_out": w_out, "w_ada": w_ada, "cst": cst})
    return in_maps


_NC_CACHE = {}


def kernel(**inputs):
    in_maps = _pack_inputs(inputs)
    if "nc" not in _NC_CACHE:
        _NC_CACHE["nc"] = build_program()
    nc = _NC_CACHE["nc"]
    res = run_bass_kernel_spmd(nc, in_maps, core_ids=list(range(NCORE)))
    out = np.empty((1, SEQ, DM), np.float32)
    for core in range(NCORE):
        out[0, core * TC:(core + 1) * TC, :] = res.results[core]["outT"].T
    return out
```
